# Optimizing a Trainium2 kernel written in Bass

```python
import math
import jax, jax.numpy as jnp
from jax import lax
import numpy as np

D_MODEL = 1024
BATCH = 2
SEQ = 8192
DEPTH = 1

N_META = 16
N_HEADS = 8
HEAD_DIM = 64
V_DIM = 2 * HEAD_DIM
QK_WIDTH = N_HEADS * 2 * HEAD_DIM
ATTN_WIDTH = N_HEADS * V_DIM
POOL_GROUPS = 4
POOL_WINDOWS = (2, 4, 8, 16)
POOL_WIDTH = 512
POOL_GDIM = POOL_WIDTH // POOL_GROUPS
N_BRANCH = 2
IN_COLS = 2 * QK_WIDTH + ATTN_WIDTH + POOL_WIDTH + N_BRANCH * D_MODEL
D_FF = 2816
CONV_WIDTH = 3
ROPE_THETA = 10000.0
EPS = 1e-6
Q_BLOCK = 128

kernel_name = "hybrid_diffattn_pool_convffn"


def rmsnorm(x, g):
    xf = x.astype(jnp.float32)
    r = lax.rsqrt(jnp.mean(xf * xf, axis=-1, keepdims=True) + EPS)
    return (xf * r).astype(x.dtype) * g


def rope_tables(length):
    pos = jnp.arange(length, dtype=jnp.float32)
    inv = 1.0 / (ROPE_THETA ** (jnp.arange(0, HEAD_DIM, 2, dtype=jnp.float32) / HEAD_DIM))
    ang = pos[:, None] * inv[None, :]
    return jnp.cos(ang), jnp.sin(ang)


def apply_rope(t, cos, sin):
    c = cos[None, :, None, None, :].astype(t.dtype)
    s = sin[None, :, None, None, :].astype(t.dtype)
    t1, t2 = jnp.split(t, 2, axis=-1)
    return jnp.concatenate([t1 * c - t2 * s, t2 * c + t1 * s], axis=-1)


def diff_attention(q, k, v, lam_full, g_subln, lam_init):
    B, L = q.shape[0], q.shape[1]
    nblk = L // Q_BLOCK
    scale = 1.0 / math.sqrt(HEAD_DIM)
    qb = q.reshape(B, nblk, Q_BLOCK, N_HEADS, 2, HEAD_DIM).transpose(1, 0, 2, 3, 4, 5)
    key_pos = jnp.arange(L)

    def block(args):
        qi, i = args
        s = jnp.einsum('bqhcd,bkhcd->bhcqk', qi, k,
                       preferred_element_type=jnp.float32) * scale
        qpos = i * Q_BLOCK + jnp.arange(Q_BLOCK)
        mask = key_pos[None, :] <= qpos[:, None]
        s = jnp.where(mask[None, None, None], s, -jnp.inf)
        p = jax.nn.softmax(s, axis=-1)
        a = p[:, :, 0] - lam_full * p[:, :, 1]
        return jnp.einsum('bhqk,bkhe->bqhe', a.astype(v.dtype), v)

    out = lax.map(block, (qb, jnp.arange(nblk)))
    out = out.transpose(1, 0, 2, 3, 4).reshape(B, L, N_HEADS, V_DIM)
    out = rmsnorm(out, g_subln) * (1.0 - lam_init)
    return out.reshape(B, L, ATTN_WIDTH)


def multiscale_pool(u, w_grp, scale):
    B, L = u.shape[0], u.shape[1]
    uf = u.astype(jnp.float32).reshape(B, L, POOL_GROUPS, POOL_GDIM)
    cs = jnp.concatenate([jnp.zeros((B, 1, POOL_GROUPS, POOL_GDIM), jnp.float32),
                          jnp.cumsum(uf, axis=1)], axis=1)
    t = jnp.arange(L)
    win = jnp.array(POOL_WINDOWS, dtype=jnp.int32)
    lo = jnp.maximum(t[:, None] + 1 - win[None, :], 0)
    gidx = jnp.broadcast_to(jnp.arange(POOL_GROUPS)[None, :], lo.shape)
    sums = cs[:, 1:] - cs[:, lo, gidx]
    count = (t[:, None] + 1 - lo).astype(jnp.float32)
    pooled = (sums / count[None, :, :, None] - uf).astype(u.dtype)
    mixed = jnp.einsum('blgc,gcd->blgd', pooled, w_grp)
    return mixed.reshape(B, L, POOL_WIDTH) * scale


def causal_dwconv(u, w, b):
    C = u.shape[-1]
    y = lax.conv_general_dilated(u, w.reshape(CONV_WIDTH, 1, C).astype(u.dtype),
                                 window_strides=(1,), padding=[(CONV_WIDTH - 1, 0)],
                                 dimension_numbers=('NWC', 'WIO', 'NWC'),
                                 feature_group_count=C)
    return y + b


def setup_inputs(seed: int = 0) -> dict:
    key = jax.random.key(seed)
    ks = jax.random.split(key, 17)
    f32 = jnp.float32
    nrm = lambda k, shp, s: jax.random.normal(k, shp, f32) * s
    return {
        "x": nrm(ks[0], (BATCH, SEQ, D_MODEL), 1.0),
        "meta_tokens": nrm(ks[1], (N_META, D_MODEL), 1.0),
        "g_mix": 1.0 + nrm(ks[2], (DEPTH, D_MODEL), 0.02),
        "w_in": nrm(ks[3], (DEPTH, D_MODEL, IN_COLS), D_MODEL ** -0.5),
        "lam": nrm(ks[4], (DEPTH, 4, HEAD_DIM), 0.1),
        "g_subln": 1.0 + nrm(ks[5], (DEPTH, V_DIM), 0.02),
        "w_pool_grp": nrm(ks[6], (DEPTH, POOL_GROUPS, POOL_GDIM, POOL_GDIM), POOL_GDIM ** -0.5),
        "pool_scale": 1.0 + nrm(ks[7], (DEPTH, POOL_WIDTH), 0.02),
        "w_attn_br": nrm(ks[8], (DEPTH, ATTN_WIDTH, D_MODEL), ATTN_WIDTH ** -0.5),
        "w_pool_br": nrm(ks[9], (DEPTH, POOL_WIDTH, D_MODEL), POOL_WIDTH ** -0.5),
        "w_out": nrm(ks[10], (DEPTH, D_MODEL, D_MODEL), D_MODEL ** -0.5),
        "g_ffn": 1.0 + nrm(ks[11], (DEPTH, D_MODEL), 0.02),
        "w_up": nrm(ks[12], (DEPTH, D_MODEL, 2 * D_FF), D_MODEL ** -0.5),
        "conv_w": nrm(ks[13], (DEPTH, CONV_WIDTH, 2 * D_FF), CONV_WIDTH ** -0.5),
        "conv_b": nrm(ks[14], (DEPTH, 2 * D_FF), 0.02),
        "w_down": nrm(ks[15], (DEPTH, D_FF, D_MODEL), D_FF ** -0.5),
        "g_final": 1.0 + nrm(ks[16], (D_MODEL,), 0.02),
    }


def reference(x, meta_tokens, g_mix, w_in, lam, g_subln, w_pool_grp, pool_scale,
              w_attn_br, w_pool_br, w_out, g_ffn, w_up, conv_w, conv_b, w_down, g_final):
    B = x.shape[0]
    L = N_META + SEQ
    L_pad = ((L + Q_BLOCK - 1) // Q_BLOCK) * Q_BLOCK
    meta = jnp.broadcast_to(meta_tokens.astype(x.dtype)[None], (B, N_META, D_MODEL))
    pad = jnp.zeros((B, L_pad - L, D_MODEL), x.dtype)
    h = jnp.concatenate([meta, x, pad], axis=1)
    cos, sin = rope_tables(L_pad)
    splits = [QK_WIDTH, 2 * QK_WIDTH, 2 * QK_WIDTH + ATTN_WIDTH,
              2 * QK_WIDTH + ATTN_WIDTH + POOL_WIDTH,
              2 * QK_WIDTH + ATTN_WIDTH + POOL_WIDTH + D_MODEL]

    for layer in range(DEPTH):
        hn = rmsnorm(h, g_mix[layer])
        proj = hn @ w_in[layer]
        q, k, v, u, ga, gp = jnp.split(proj, splits, axis=-1)
        q = apply_rope(q.reshape(B, L_pad, N_HEADS, 2, HEAD_DIM), cos, sin)
        k = apply_rope(k.reshape(B, L_pad, N_HEADS, 2, HEAD_DIM), cos, sin)
        v = v.reshape(B, L_pad, N_HEADS, V_DIM)
        lam_init = 0.8 - 0.6 * math.exp(-0.3 * layer)
        lp = lam[layer].astype(jnp.float32)
        lam_full = (jnp.exp(jnp.sum(lp[0] * lp[1])) - jnp.exp(jnp.sum(lp[2] * lp[3]))
                    + lam_init)
        attn = diff_attention(q, k, v, lam_full, g_subln[layer], lam_init)
        pool = multiscale_pool(u, w_pool_grp[layer], pool_scale[layer])
        merged = (jax.nn.sigmoid(ga) * (attn @ w_attn_br[layer])
                  + jax.nn.sigmoid(gp) * (pool @ w_pool_br[layer]))
        h = h + merged @ w_out[layer]
        hn = rmsnorm(h, g_ffn[layer])
        up = causal_dwconv(hn @ w_up[layer], conv_w[layer], conv_b[layer])
        val, gate = jnp.split(up, 2, axis=-1)
        h = h + (jax.nn.silu(gate) * val) @ w_down[layer]

    h = rmsnorm(h, g_final)
    return h[:, N_META:N_META + SEQ]
```

```python
import contextlib
import numpy as np
import concourse.bass as bass
import concourse.mybir as mybir
from concourse.bass_utils import run_bass_kernel_spmd

F32 = mybir.dt.float32
BF16 = mybir.dt.bfloat16
AF = mybir.ActivationFunctionType
ALU = mybir.AluOpType

ENGS = ("pe", "act", "dve", "pool", "sp")

D = 1024
SEQ = 8192
NMETA = 16
NPOS = NMETA + SEQ
NH = 8
DFF = 2816
NCH_FF = 44
EPS = 1e-6
ROWS = 8
BQ = 256
NMAIN = ROWS * BQ
NH17 = ROWS * 17
NHQ = ROWS * 2
NOWN = NMAIN + NH17 + NHQ
NQ = NMAIN + NHQ
LAM_INIT = 0.2


class Buf:
    def __init__(self, t, name, exclusive=False):
        self.t = t
        self.name = name
        self.regions = {}
        self.exclusive = exclusive

    def __getitem__(self, idx):
        return self.t[idx]


class _Region:
    __slots__ = ("last_write", "reads", "sem", "ndma")

    def __init__(self):
        self.last_write = None
        self.reads = []
        self.sem = None
        self.ndma = 0


class _Op:
    __slots__ = ("eng", "fn", "deps", "is_dma", "signal", "region", "dma_idx", "sig_idx", "idx")


def R(buf, key=None):
    return (buf, key)


class Prog:
    def __init__(self, nc):
        self.nc = nc
        self.ops = []
        self.stack = contextlib.ExitStack()
        self.last_on_eng = {}
        self.barrier_deps = {}

    def dram(self, name, shape, dtype, kind="Internal"):
        t = self.nc.dram_tensor(name, list(shape), dtype, kind=kind)
        return Buf(t.ap(), name)

    def _regs(self, acc):
        buf, key = acc
        if key is None:
            out = [buf.regions.setdefault(None, _Region())]
            out += [r for k, r in buf.regions.items() if k is not None]
            return out
        out = [buf.regions.setdefault(key, _Region())]
        if None in buf.regions:
            out.append(buf.regions[None])
        return out

    def barrier(self):
        deps = set(self.last_on_eng.values())
        seen = set()
        for o in self.ops:
            if o.is_dma:
                seen.add(id(o.region))
        last_dma = {}
        for o in self.ops:
            if o.is_dma:
                last_dma[id(o.region)] = o.idx
        deps.update(last_dma.values())
        for e in ENGS:
            self.barrier_deps[e] = set(deps) | self.barrier_deps.get(e, set())

    def op(self, eng, fn, reads=(), writes=(), dma=False, no_waw=False):
        o = _Op()
        o.eng, o.fn, o.is_dma, o.signal = eng, fn, dma, False
        o.idx = len(self.ops)
        reads = list(reads)
        writes = list(writes)
        excl = [a for a in reads if a[0].exclusive]
        if excl:
            reads = [a for a in reads if not a[0].exclusive]
            writes = writes + [a for a in excl if a not in writes]
        deps = set()
        for acc in reads:
            for r in self._regs(acc):
                if r.last_write is not None:
                    deps.add(r.last_write)
        for acc in writes:
            for r in self._regs(acc):
                if r.last_write is not None and not no_waw:
                    deps.add(r.last_write)
                deps.update(r.reads)
        if eng in self.barrier_deps:
            deps.update(self.barrier_deps.pop(eng))
        deps.discard(o.idx)
        o.deps = deps
        o.region = None
        if dma:
            buf, key = writes[0]
            o.region = buf.regions.setdefault(key, _Region())
        for acc in reads:
            buf, key = acc
            rl = buf.regions.setdefault(key, _Region()).reads
            if not dma:
                for i_, prev in enumerate(rl):
                    po = self.ops[prev]
                    if (not po.is_dma) and po.eng == eng:
                        rl[i_] = o.idx
                        break
                else:
                    rl.append(o.idx)
            else:
                rl.append(o.idx)
        for acc in writes:
            buf, key = acc
            reg = buf.regions.setdefault(key, _Region())
            reg.reads = []
            reg.last_write = o.idx
            if key is None:
                for k, r in buf.regions.items():
                    if k is not None:
                        r.last_write = o.idx
                        r.reads = []
        self.ops.append(o)
        if not dma:
            self.last_on_eng[eng] = o.idx
        return o

    def emit(self):
        nc = self.nc
        ops = self.ops
        for o in ops:
            for d in o.deps:
                od = ops[d]
                if od.eng == "pe" and o.eng == "pe" and not od.is_dma:
                    continue
                od.signal = True
        cnt = {e: 0 for e in ENGS}
        for o in ops:
            if o.is_dma:
                o.region.ndma += 1
                o.dma_idx = o.region.ndma
            elif o.signal:
                cnt[o.eng] += 1
                o.sig_idx = cnt[o.eng]
        self.sig_counts = cnt
        sems = {e: self.stack.enter_context(nc.semaphore(f"s_{e}")) for e in ENGS}
        nreg = 0
        for o in ops:
            if o.is_dma and o.region.sem is None:
                nreg += 1
                o.region.sem = self.stack.enter_context(nc.semaphore(f"d_{nreg}"))
        self.n_dma_sems = nreg
        per_eng = {e: [o for o in ops if o.eng == e] for e in ENGS}

        def run(ename, eng):
            waited = {}
            for o in per_eng[ename]:
                for d in sorted(o.deps):
                    od = ops[d]
                    if od.is_dma:
                        key = ("d", id(od.region))
                        val = 16 * od.dma_idx
                        sem = od.region.sem
                    else:
                        if od.eng == "pe" and ename == "pe":
                            continue
                        key = ("e", od.eng)
                        val = od.sig_idx
                        sem = sems[od.eng]
                    if waited.get(key, 0) >= val:
                        continue
                    waited[key] = val
                    eng.wait_ge(sem, val)
                ins = o.fn(eng)
                if o.is_dma:
                    ins.then_inc(o.region.sem, 16)
                elif o.signal:
                    ins.then_inc(sems[ename], 1)
            return waited

        with nc.Block() as block:
            @block.tensor
            def _(e):
                run("pe", e)

            @block.scalar
            def _(e):
                run("act", e)

            @block.vector
            def _(e):
                run("dve", e)

            @block.gpsimd
            def _(e):
                run("pool", e)

            @block.sync
            def _(e):
                w = run("sp", e)
                seen = set()
                for o in ops:
                    if o.is_dma and id(o.region) not in seen:
                        seen.add(id(o.region))
                        val = 16 * o.region.ndma
                        if w.get(("d", id(o.region)), 0) < val:
                            e.wait_ge(o.region.sem, val)

    def close(self):
        self.stack.close()


def build_program(dbg=False, stop=None):
    nc = bass.Bass("TRN2", target_bir_lowering=False)
    P = Prog(nc)
    IN = "ExternalInput"

    class _Stop(Exception):
        pass

    def finish():
        P.emit()
        P.close()
        return nc, P

    xT_all = P.dram("xT_all", [D, NMETA + SEQ], F32, IN)
    xT_own = P.dram("xT_own", [D, NOWN], F32, IN)
    cosK = P.dram("cosK", [NPOS, 32], F32, IN)
    sinK = P.dram("sinK", [NPOS, 32], F32, IN)
    cosQ = P.dram("cosQ", [NQ, 32], F32, IN)
    sinQ = P.dram("sinQ", [NQ, 32], F32, IN)
    maskM = P.dram("maskM", [128, 8 * BQ], F32, IN)
    maskH = P.dram("maskH", [128, 65 * NHQ], F32, IN)
    cnt16 = P.dram("cnt16", [128, NHQ], F32, IN)
    w_in = P.dram("w_in", [D, 5632], F32, IN)
    lam = P.dram("lam", [1, 256], F32, IN)
    vecs = P.dram("vecs", [128, 8 + 8 + 8 + 4 + 1 + 44 * 3 + 44], F32, IN)
    w_grp = P.dram("w_grp", [4, 128, 128], F32, IN)
    w_attn_br = P.dram("w_attn_br", [D, D], F32, IN)
    w_pool_br = P.dram("w_pool_br", [512, D], F32, IN)
    w_out = P.dram("w_out", [D, D], F32, IN)
    w_up = P.dram("w_up", [D, 5632], F32, IN)
    w_down = P.dram("w_down", [DFF, D], F32, IN)
    yT = P.dram("yT", [D, NMAIN], F32, "ExternalOutput")
    kT_scr = P.dram("kT_scr", [NH, 128, NPOS], BF16)
    v_scr = P.dram("v_scr", [NH, 128, 64, 128], BF16)
    vm_scr = P.dram("vm_scr", [NH, 16, 128], BF16)
    h1_scr = P.dram("h1_scr", [D, NQ], F32, "ExternalOutput" if dbg else "Internal")
    dbgT = {}
    if dbg:
        dbgT["QT"] = P.dram("dbg_QT", [128, NH * NQ], BF16, "ExternalOutput")
        dbgT["attnT"] = P.dram("dbg_attnT", [128, NH * NQ], BF16, "ExternalOutput")
        dbgT["kT"] = P.dram("dbg_kT", [128, NPOS], BF16, "ExternalOutput")
        dbgT["v"] = P.dram("dbg_v", [128, 64 * 128], BF16, "ExternalOutput")

    ARENA_F32 = 51200
    arena_t = P.stack.enter_context(nc.sbuf_tensor("arena", [128, ARENA_F32], F32))
    psum_t = P.stack.enter_context(nc.psum_tensor("psum", [128, 4096], F32))
    PS = Buf(psum_t, "psum", exclusive=True)

    class Arena:
        def __init__(self):
            self.off = 0
            self.n = 0

        def alloc(self, name, shape, dtype):
            esz = 4 if dtype == F32 else 2
            nfree = int(np.prod(shape[1:]))
            nbytes = (nfree * esz + 31) // 32 * 32
            o4 = self.off // 4
            n4 = nbytes // 4
            assert o4 + n4 <= ARENA_F32, f"arena overflow at {name}: {self.off + nbytes}"
            ap = arena_t[:, o4:o4 + n4]
            if dtype != F32:
                ap = ap.bitcast(dtype)
            ap = ap[:, 0:nfree]
            if len(shape) > 2:
                names = " ".join(f"d{i}" for i in range(len(shape) - 1))
                kw = {f"d{i}": shape[i + 1] for i in range(len(shape) - 1)}
                ap = ap.rearrange(f"p ({names}) -> p {names}", **kw)
            self.off += nbytes
            self.n += 1
            return Buf(ap, f"{name}_{self.n}")

    A = Arena()

    def psb(bank, ncol=512, col0=0, dtype=F32):
        ap = psum_t[:, bank * 512 + col0: bank * 512 + col0 + ncol]
        return ap

    bank_rr = [0]

    def next_bank(lo=0, hi=8):
        b = lo + bank_rr[0] % (hi - lo)
        bank_rr[0] += 1
        return b

    def dma(out_b, out_ap, in_b, in_ap, eng="sp", okey=None, ikey=None, no_waw=False):
        P.op(eng, lambda e: e.dma_start(out=out_ap, in_=in_ap), reads=[R(in_b, ikey)],
             writes=[R(out_b, okey)], dma=True, no_waw=no_waw)

    def mm(out_ap, lhsT, rhs, start, stop, reads, bank):
        P.op("pe", lambda e: e.matmul(out_ap, lhsT=lhsT, rhs=rhs, start=start, stop=stop),
             reads=reads, writes=[R(PS, bank)])

    def act(out_ap, in_ap, func, reads, writes, scale=1.0, bias=None):
        if bias is None:
            P.op("act", lambda e: e.activation(out=out_ap, in_=in_ap, func=func, scale=scale),
                 reads=reads, writes=writes)
        else:
            P.op("act", lambda e: e.activation(out=out_ap, in_=in_ap, func=func, scale=scale, bias=bias),
                 reads=reads, writes=writes)

    def tt(eng, out_ap, in0, in1, op, reads, writes):
        P.op(eng, lambda e: e.tensor_tensor(out=out_ap, in0=in0, in1=in1, op=op), reads=reads, writes=writes)

    def ts(eng, out_ap, in0, s1, s2, op0, op1, reads, writes):
        if op1 is None:
            P.op(eng, lambda e: e.tensor_scalar(out=out_ap, in0=in0, scalar1=s1, scalar2=None, op0=op0),
                 reads=reads, writes=writes)
        else:
            P.op(eng, lambda e: e.tensor_scalar(out=out_ap, in0=in0, scalar1=s1, scalar2=s2, op0=op0, op1=op1),
                 reads=reads, writes=writes)

    def stt(eng, out_ap, in0, scalar, in1, op0, op1, reads, writes):
        P.op(eng, lambda e: e.scalar_tensor_tensor(out=out_ap, in0=in0, scalar=scalar, in1=in1, op0=op0, op1=op1),
             reads=reads, writes=writes)

    def tr(out_ap, in_ap, id_ap, reads, bank):
        P.op("pe", lambda e: e.transpose(out_ap, in_ap, id_ap), reads=reads, writes=[R(PS, bank)])

    def cp(eng, out_ap, in_ap, reads, writes):
        P.op(eng, lambda e: e.tensor_copy(out=out_ap, in_=in_ap), reads=reads, writes=writes)

    def rsqrt_act(out_b, out_ap, in_ap, in_reads, tmp_b, tmp_ap, inv_n):
        act(tmp_ap, in_ap, AF.Ln, in_reads, [R(tmp_b)], scale=inv_n, bias=EPS)
        act(out_ap, tmp_ap, AF.Exp, [R(tmp_b)], [R(out_b)], scale=-0.5)

    ident = A.alloc("ident", [128, 128], BF16)
    ones = A.alloc("ones", [128, 128], BF16)
    idf = A.alloc("idf", [128, 128], F32)
    vec = A.alloc("vec", [128, 205], F32)
    lamb = A.alloc("lamb", [128, 256], F32)
    lprod = A.alloc("lprod", [128, 128], F32)
    lsum = A.alloc("lsum", [128, 2], F32)
    neglam = A.alloc("neglam", [128, 1], F32)
    gsub08 = A.alloc("gsub08", [128, 1], F32)
    pscw = A.alloc("pscw", [128, 4], F32)
    mM = A.alloc("mM", [128, 8, BQ], BF16)
    mH = A.alloc("mH", [128, 65, NHQ], BF16)
    c16 = A.alloc("c16", [128, NHQ], F32)
    G_MIX, G_FFN, G_FIN, PSC, GSUB, CW, CB = 0, 8, 16, 24, 28, 29, 29 + 132

    P.op("pool", lambda e: e.memset(idf[:], 1.0), writes=[R(idf)])
    P.op("pool", lambda e: e.affine_select(out=idf[:], in_=idf[:], pattern=[[-1, 128]], compare_op=ALU.is_equal,
                                           fill=0.0, base=0, channel_multiplier=1), reads=[R(idf)], writes=[R(idf)])
    cp("dve", ident[:], idf[:], [R(idf)], [R(ident)])
    P.op("dve", lambda e: e.memset(ones[:], 1.0), writes=[R(ones)])
    dma(vec, vec[:], vecs, vecs[:])
    dma(lamb, lamb[:], lam, lam[:].to_broadcast([128, 256]))
    dma(c16, c16[:], cnt16, cnt16[:])
    dma(mM, mM[:], maskM, maskM[:].rearrange("p (i q) -> p i q", i=8), eng="pool")
    dma(mH, mH[:], maskH, maskH[:].rearrange("p (b q) -> p b q", b=65), eng="pool")
    lv = lamb[:].rearrange("p (a b d) -> p a b d", a=2, b=2)
    tt("dve", lprod[:].rearrange("p (a d) -> p a d", a=2), lv[:, :, 0, :], lv[:, :, 1, :], ALU.mult,
       [R(lamb)], [R(lprod)])
    P.op("dve", lambda e: e.reduce_sum(out=lsum[:], in_=lprod[:].rearrange("p (a d) -> p a d", a=2),
                                       axis=mybir.AxisListType.X), reads=[R(lprod)], writes=[R(lsum)])
    act(lsum[:], lsum[:], AF.Exp, [R(lsum)], [R(lsum)])
    stt("dve", neglam[:], lsum[:, 1:2], -LAM_INIT, lsum[:, 0:1], ALU.add, ALU.subtract, [R(lsum)], [R(neglam)])
    ts("dve", gsub08[:], vec[:, GSUB:GSUB + 1], 1.0 - LAM_INIT, None, ALU.mult, None, [R(vec)], [R(gsub08)])
    for g in range(4):
        ts("dve", pscw[:, g:g + 1], vec[:, PSC + g:PSC + g + 1], 1.0 / (2 ** (g + 1)), None, ALU.mult, None,
           [R(vec)], [R(pscw, g)])
    CONST_END = A.off

    QT = A.alloc("QT", [128, NH, NQ], BF16)
    QT_END = A.off
    attnT = A.alloc("attnT", [128, NH, NQ], BF16)
    ATT_END = A.off

    def w_view(wb, lo, hi):
        return wb[:].rearrange("(c p) n -> p c n", p=128)[:, :, lo:hi]

    def proj_phase(tiles, W_list, sink):
        xt = [A.alloc("xt", [128, 8, 512], F32) for _ in range(2)]
        xb = [A.alloc("xb", [128, 8, 512], BF16) for _ in range(2)]
        xsq1 = A.alloc("xsq", [128, 8, 512], BF16)
        xsq = [xsq1, xsq1]
        ct = [A.alloc("ct", [128, 4, 32], F32) for _ in range(2)]
        st = [A.alloc("st", [128, 4, 32], F32) for _ in range(2)]
        r4 = [A.alloc("r4", [128, 4], F32) for _ in range(2)]
        lnr = [A.alloc("lnr", [128, 4], F32) for _ in range(2)]
        kf = [A.alloc("kf", [128, 1024], F32) for _ in range(2)]
        tA1 = A.alloc("tA", [128, 1024], F32)
        tB1 = A.alloc("tB", [128, 1024], F32)
        tA = [tA1, tA1]
        tB = [tB1, tB1]
        kb16 = [A.alloc("kb16", [128, 1024], BF16) for _ in range(2)]
        gmix_bc = vec[:, G_MIX:G_MIX + 8].unsqueeze(2)
        bigs = [(0, 1), (2, 3), (4, 5)]
        big_i = [0]
        RB, TB = 6, 7
        bcount = [0]

        def load(ti):
            src, col0, ncols, blocks, cosd, sind, row0 = tiles[ti]
            s = ti % 2
            dma(xt[s], xt[s][:, :, 0:ncols], src,
                src[:].rearrange("(c p) t -> p c t", p=128)[:, :, col0:col0 + ncols])
            for bi, (boff, nb) in enumerate(blocks):
                dma(ct[s], ct[s][0:nb, bi, :], cosd, cosd[row0 + boff:row0 + boff + nb, :], okey=bi)
                dma(st[s], st[s][0:nb, bi, :], sind, sind[row0 + boff:row0 + boff + nb, :], okey=bi)

        load(0)
        for ti in range(len(tiles)):
            src, col0, ncols, blocks, cosd, sind, row0 = tiles[ti]
            s = ti % 2
            if ti + 1 < len(tiles):
                load(ti + 1)
            tt("dve", xb[s][:, :, 0:ncols], xt[s][:, :, 0:ncols], gmix_bc.to_broadcast([128, 8, ncols]), ALU.mult,
               [R(xt[s]), R(vec)], [R(xb[s])])
            act(xsq[s][:, :, 0:ncols], xt[s][:, :, 0:ncols], AF.Square, [R(xt[s])], [R(xsq[s])])
            for bi, (boff, nb) in enumerate(blocks):
                for c in range(8):
                    mm(psb(RB)[0:nb, bi:bi + 1], xsq[s][:, c, boff:boff + nb], ones[:, 0:1], c == 0, c == 7,
                       [R(xsq[s]), R(ones)], RB)
            nbl = len(blocks)
            nb0 = blocks[0][1]
            act(lnr[s][0:nb0, 0:nbl], psb(RB)[0:nb0, 0:nbl], AF.Ln, [R(PS, RB)], [R(lnr[s])], scale=1.0 / D, bias=EPS)
            act(r4[s][0:nb0, 0:nbl], lnr[s][0:nb0, 0:nbl], AF.Exp, [R(lnr[s])], [R(r4[s])], scale=-0.5)
            for bi, (boff, nb) in enumerate(blocks):
                for wi, (Wsb, kind) in enumerate(W_list):
                    bk = bigs[big_i[0] % 3]
                    big_i[0] += 1
                    for half in range(2):
                        for c in range(8):
                            mm(psb(bk[half])[0:nb, :], xb[s][:, c, boff:boff + nb],
                               Wsb[:, c, half * 512:(half + 1) * 512], c == 0, c == 7,
                               [R(xb[s]), R(Wsb)], bk[half])
                    big_ap = psum_t[0:nb, bk[0] * 512:bk[0] * 512 + 1024]
                    q = bcount[0] % 2
                    bcount[0] += 1
                    if kind == "plain":
                        sink(ti, wi, bi, boff, nb, (big_ap, [R(PS, bk[0]), R(PS, bk[1])], r4[s][0:nb, bi:bi + 1], R(r4[s])))
                        continue
                    act(kf[q][0:nb, :], big_ap, AF.Copy, [R(PS, bk[0]), R(PS, bk[1]), R(r4[s])], [R(kf[q])],
                        scale=r4[s][0:nb, bi:bi + 1])
                    kv = kf[q][0:nb, :].rearrange("p (a b d) -> p a b d", a=16, b=2)
                    av = tA[q][0:nb, :].rearrange("p (a b d) -> p a b d", a=16, b=2)
                    bv = tB[q][0:nb, :].rearrange("p (a b d) -> p a b d", a=16, b=2)
                    ov = kb16[q][0:nb, :].rearrange("p (a b d) -> p a b d", a=16, b=2)
                    cosb = ct[s][0:nb, bi, :]
                    sinb = st[s][0:nb, bi, :]
                    tt("pool", av, kv, cosb.unsqueeze(1).unsqueeze(1).to_broadcast([nb, 16, 2, 32]), ALU.mult,
                       [R(kf[q]), R(ct[s], bi)], [R(tA[q])])
                    sb_ = sinb.unsqueeze(1).to_broadcast([nb, 16, 32])
                    tt("dve", bv[:, :, 0, :], kv[:, :, 1, :], sb_, ALU.mult, [R(kf[q]), R(st[s], bi)], [R(tB[q], 0)])
                    tt("dve", bv[:, :, 1, :], kv[:, :, 0, :], sb_, ALU.mult, [R(kf[q]), R(st[s], bi)], [R(tB[q], 1)])
                    tt("dve", ov[:, :, 0, :], av[:, :, 0, :], bv[:, :, 0, :], ALU.subtract,
                       [R(tA[q]), R(tB[q], 0)], [R(kb16[q], 0)])
                    tt("dve", ov[:, :, 1, :], av[:, :, 1, :], bv[:, :, 1, :], ALU.add,
                       [R(tA[q]), R(tB[q], 1)], [R(kb16[q], 1)])
                    tps = psb(TB).bitcast(BF16)
                    for h in range(NH):
                        tr(tps[:, h * 128:h * 128 + nb], kb16[q][0:nb, h * 128:(h + 1) * 128], ident[0:nb, 0:nb],
                           [R(kb16[q], 0), R(kb16[q], 1), R(ident)], TB)
                    tview = tps[:, 0:1024].rearrange("p (h t) -> p h t", h=NH)[:, :, 0:nb]
                    sink(ti, wi, bi, boff, nb, (tview, [R(PS, TB)], None, None))

    A.off = QT_END
    Wk = A.alloc("Wk", [128, 8, 1024], BF16)
    Wv = A.alloc("Wv", [128, 8, 1024], BF16)
    dma(Wk, Wk[:], w_in, w_view(w_in, 1024, 2048), eng="pool")
    dma(Wv, Wv[:], w_in, w_view(w_in, 2048, 3072), eng="pool")
    kst = [A.alloc("kst", [128, NH, 512], BF16) for _ in range(2)]
    vst = [A.alloc("vst", [128, 4, 1024], BF16) for _ in range(2)]
    tilesA = [(xT_all, 0, 16, [(0, 16)], cosK, sinK, 0)]
    for t in range(16):
        tilesA.append((xT_all, 16 + 512 * t, 512, [(128 * b, 128) for b in range(4)], cosK, sinK, 16 + 512 * t))

    def sinkA(ti, wi, bi, boff, nb, data):
        s = ti % 2
        src_ap, src_reads, rcol, rread = data
        if wi == 0:
            act(kst[s][:, :, boff:boff + nb], src_ap, AF.Copy, src_reads, [R(kst[s], bi)])
            last = (bi == len(tilesA[ti][3]) - 1)
            if last:
                ncols = tilesA[ti][2]
                col0 = tilesA[ti][1]
                dma(kT_scr, kT_scr[:].rearrange("h f t -> f h t")[:, :, col0:col0 + ncols],
                    kst[s], kst[s][:, :, 0:ncols], no_waw=True)
        else:
            act(vst[s][0:nb, bi, :], src_ap, AF.Copy, src_reads + [rread], [R(vst[s], bi)], scale=rcol)
            if ti == 0:
                dma(vm_scr, vm_scr[:].rearrange("h t e -> t h e"),
                    vst[s], vst[s][0:16, 0, :].rearrange("t (h e) -> t h e", h=NH), ikey=bi)
            else:
                kb = 4 * (ti - 1) + bi
                dma(v_scr, v_scr[:, :, kb, :].rearrange("h t e -> t h e"),
                    vst[s], vst[s][:, bi, :].rearrange("t (h e) -> t h e", h=NH), ikey=bi, no_waw=True)

    proj_phase(tilesA, [(Wk, "rope"), (Wv, "plain")], sinkA)
    if stop == "A":
        if dbg:
            KTd = A.alloc("KTd", [128, NPOS], BF16)
            dma(KTd, KTd[:], kT_scr, kT_scr[0])
            dma(dbgT["kT"], dbgT["kT"][:], KTd, KTd[:])
        return finish()

    P.barrier()
    A.off = QT_END
    Wq = A.alloc("Wq", [128, 8, 1024], BF16)
    dma(Wq, Wq[:], w_in, w_view(w_in, 0, 1024), eng="pool")
    tilesB = []
    for t in range(4):
        tilesB.append((xT_own, 512 * t, 512, [(128 * b, 128) for b in range(4)], cosQ, sinQ, 512 * t))
    tilesB.append((xT_own, NMAIN + NH17, NHQ, [(0, NHQ)], cosQ, sinQ, NMAIN))

    def sinkB(ti, wi, bi, boff, nb, data):
        src_ap, src_reads, _, _ = data
        q0 = (512 * ti if ti < 4 else NMAIN) + boff
        act(QT[:, :, q0:q0 + nb], src_ap, AF.Copy, src_reads, [R(QT, (ti, bi))])

    proj_phase(tilesB, [(Wq, "rope")], sinkB)
    if dbg:
        dma(dbgT["QT"], dbgT["QT"][:], QT, QT[:].rearrange("p h q -> p (h q)"))

    if stop == "B":
        return finish()
    P.barrier()
    A.off = ATT_END
    KT = [A.alloc("KT", [128, NPOS], BF16) for _ in range(2)]
    VV = [A.alloc("VV", [128, 64, 128], BF16) for _ in range(2)]
    VM = [A.alloc("VM", [128, 128], BF16) for _ in range(2)]
    Pt = [A.alloc("Pt", [128, 2, 512], BF16) for _ in range(2)]
    Osb = A.alloc("Osb", [128, 512], F32)
    Lsb = A.alloc("Lsb", [128, 512], F32)
    t0b = A.alloc("t0b", [128, 256], F32)
    t1b = A.alloc("t1b", [128, 256], F32)
    ob = A.alloc("ob", [128, 256], F32)
    osq = A.alloc("osq", [128, 256], BF16)
    lnb = A.alloc("lnb", [128, 256], F32)
    rsb = A.alloc("rsb", [128, 256], F32)
    SBK = [(0, 1), (2, 3)]
    OB, LB, EB = 4, 5, 6

    def load_head(h):
        s = h % 2
        dma(KT[s], KT[s][:], kT_scr, kT_scr[h])
        dma(VV[s], VV[s][:], v_scr, v_scr[h])
        dma(VM[s], VM[s][0:16, :], vm_scr, vm_scr[h])

    def attention_tile(h, q0, nq, nkb, kps, mask_of):
        s = h % 2
        kt, vv, vm = KT[s], VV[s], VM[s]
        q_r = [R(QT)]
        steps = [(-1, 1)] + [(kb0, kps) for kb0 in range(0, nkb, kps)]
        nst = len(steps)
        w = kps * nq

        def qk(si):
            kb0, n = steps[si]
            sb = SBK[si % 2]
            for kk in range(n):
                if kb0 < 0:
                    kcols, np_ = slice(0, 16), 16
                else:
                    c0 = 16 + 128 * (kb0 + kk)
                    kcols, np_ = slice(c0, c0 + 128), 128
                for comp in range(2):
                    mm(psb(sb[comp])[0:np_, kk * nq:(kk + 1) * nq], kt[comp * 64:(comp + 1) * 64, kcols],
                       QT[comp * 64:(comp + 1) * 64, h, q0:q0 + nq], True, True, q_r + [R(kt)], sb[comp])

        def expo(si):
            kb0, n = steps[si]
            sb = SBK[si % 2]
            np_ = 16 if kb0 < 0 else 128
            pin = psum_t[0:np_, sb[0] * 512:sb[0] * 512 + 1024].rearrange("p (c w) -> p c w", c=2)[:, :, 0:n * nq]
            pt = Pt[si % 2]
            act(pt[0:np_, :, 0:n * nq], pin, AF.Exp, [R(PS, sb[0]), R(PS, sb[1])], [R(pt)], scale=0.125)
            ptv = pt[0:np_, :, 0:n * nq].rearrange("p c (k q) -> p c k q", k=n)
            if kb0 < 0:
                m = mask_of(-1)
                if m is not None:
                    tt("dve", ptv[:, :, 0, :], ptv[:, :, 0, :], m[0].unsqueeze(1).to_broadcast([np_, 2, nq]), ALU.mult,
                       [R(pt), m[1]], [R(pt)])
            else:
                ms = [mask_of(kb0 + kk) for kk in range(n)]
                if all(m is not None for m in ms) and n > 1 and ms[0][2] is not None:
                    mb, mr, (mbuf, b0) = ms[0]
                    mall = mbuf[:, b0:b0 + n, :]
                    tt("dve", ptv, ptv, mall.unsqueeze(1).to_broadcast([np_, 2, n, nq]), ALU.mult,
                       [R(pt), mr], [R(pt)])
                else:
                    for kk, m in enumerate(ms):
                        if m is not None:
                            tt("dve", ptv[:, :, kk, :], ptv[:, :, kk, :],
                               m[0].unsqueeze(1).to_broadcast([np_, 2, nq]), ALU.mult, [R(pt), m[1]], [R(pt)])

        def pv(si):
            kb0, n = steps[si]
            pt = Pt[si % 2]
            np_ = 16 if kb0 < 0 else 128
            ptv = pt[0:np_, :, 0:n * nq].rearrange("p c (k q) -> p c k q", k=n)
            for kk in range(n):
                first = (si == 0 and kk == 0)
                last = (si == nst - 1 and kk == n - 1)
                lhs_v = vm[0:16, :] if kb0 < 0 else vv[:, kb0 + kk, :]
                mm(psb(OB)[:, 0:2 * nq], lhs_v, ptv[:, :, kk, :], first, last, [R(pt), R(vm if kb0 < 0 else vv)], OB)
                mm(psb(LB)[:, 0:2 * nq], ones[0:np_, :], ptv[:, :, kk, :], first, last, [R(pt), R(ones)], LB)

        qk(0)
        for si in range(nst):
            if si + 1 < nst:
                qk(si + 1)
            expo(si)
            pv(si)
        n2 = 2 * nq
        cp("dve", Lsb[:, 0:n2], psb(LB)[:, 0:n2], [R(PS, LB)], [R(Lsb)])
        cp("dve", Osb[:, 0:n2], psb(OB)[:, 0:n2], [R(PS, OB)], [R(Osb)])
        P.op("dve", lambda e: e.reciprocal(out=Lsb[:, 0:n2], in_=Lsb[:, 0:n2]), reads=[R(Lsb)], writes=[R(Lsb)])
        tt("dve", t0b[:, 0:nq], Osb[:, 0:nq], Lsb[:, 0:nq], ALU.mult, [R(Osb), R(Lsb)], [R(t0b)])
        tt("pool", t1b[:, 0:nq], Osb[:, nq:n2], Lsb[:, nq:n2], ALU.mult, [R(Osb), R(Lsb)], [R(t1b)])
        stt("dve", ob[:, 0:nq], t1b[:, 0:nq], neglam[:, 0:1], t0b[:, 0:nq], ALU.mult, ALU.add,
            [R(t1b), R(t0b), R(neglam)], [R(ob)])
        tt("pool", osq[:, 0:nq], ob[:, 0:nq], ob[:, 0:nq], ALU.mult, [R(ob)], [R(osq)])
        mm(psb(EB)[:, 0:nq], ones[:, :], osq[:, 0:nq], True, True, [R(osq), R(ones)], EB)
        rsqrt_act(rsb, rsb[:, 0:nq], psb(EB)[:, 0:nq], [R(PS, EB)], lnb, lnb[:, 0:nq], 1.0 / 128)
        stt("dve", attnT[:, h, q0:q0 + nq], ob[:, 0:nq], gsub08[:, 0:1], rsb[:, 0:nq], ALU.mult, ALU.mult,
            [R(ob), R(gsub08), R(rsb)], [R(attnT, (h, q0))])

    load_head(0)
    for h in range(NH):
        if h + 1 < NH:
            load_head(h + 1)
        for m in range(ROWS):
            nkb = 8 * m + 8

            def mask_main(blk, m=m):
                if blk < 0 or blk < 8 * m:
                    return None
                i = blk - 8 * m
                return (mM[:, i, :], R(mM), (mM, i))
            attention_tile(h, m * BQ, BQ, nkb, 2, mask_main)

        def mask_halo(blk):
            if blk < 0:
                return (mH[0:16, 0, :], R(mH), None)
            return (mH[:, 1 + blk, :], R(mH), (mH, 1 + blk))
        attention_tile(h, NMAIN, NHQ, 64, 16, mask_halo)
        if dbg and h == 0:
            dma(dbgT["kT"], dbgT["kT"][:], KT[0], KT[0][:])
            dma(dbgT["v"], dbgT["v"][:], VV[0], VV[0][:].rearrange("p b e -> p (b e)"))
    if dbg:
        dma(dbgT["attnT"], dbgT["attnT"][:], attnT, attnT[:].rearrange("p h q -> p (h q)"))

    if stop == "C":
        return finish()
    P.barrier()
    TW = 256

    def wload(name, wb, lo, hi, kch):
        Wsb = A.alloc(name, [128, kch, hi - lo], BF16)
        dma(Wsb, Wsb[:], wb, w_view(wb, lo, hi), eng="pool")
        return Wsb

    A.off = CONST_END
    Wu = wload("Wu", w_in, 3072, 3584, 8)
    Wgrp = A.alloc("Wgrp", [128, 4, 128], BF16)
    dma(Wgrp, Wgrp[:], w_grp, w_grp[:].rearrange("g c d -> c g d"), eng="pool")
    Wpb = wload("Wpb", w_pool_br, 0, 1024, 4)
    uh = A.alloc("uh", [128, 4, NH17], F32)
    h1h = A.alloc("h1h", [128, 8, NHQ], F32)
    xbq = A.alloc("xbq", [128, 8, NHQ], BF16)
    xtq = A.alloc("xtq", [128, 8, NHQ], F32)
    r1q = A.alloc("r1q", [128, NHQ], F32)
    assert A.off <= QT_END
    A.off = ATT_END
    Wga = wload("Wga", w_in, 3584, 4608, 8)
    Wgp = wload("Wgp", w_in, 4608, 5632, 8)
    Wab = wload("Wab", w_attn_br, 0, 1024, 8)
    Wo = wload("Wo", w_out, 0, 1024, 8)
    xt1 = A.alloc("xt1", [128, 8, TW], F32)
    xb1 = A.alloc("xb1", [128, 8, TW], BF16)
    xq1 = A.alloc("xq1", [128, 8, TW], BF16)
    merged = xq1
    r1 = A.alloc("r1", [128, TW], F32)
    ln1 = A.alloc("ln1", [128, TW], F32)
    ub = A.alloc("ub", [128, 4, 271], F32)
    s1 = A.alloc("s1", [128, 4, 271], F32)
    s2 = A.alloc("s2", [128, 4, 271], F32)
    pooled = A.alloc("pooled", [128, 4, TW], BF16)
    mixed = A.alloc("mixed", [128, 4, TW], BF16)
    tg = [A.alloc("tg", [128, TW], F32) for _ in range(2)]
    m1 = [A.alloc("m1", [128, TW], F32) for _ in range(2)]
    m2 = [A.alloc("m2", [128, TW], F32) for _ in range(2)]
    gmix_bc = vec[:, G_MIX:G_MIX + 8].unsqueeze(2)

    def linear(Wsb, kch, o, rhs_of, n, reads):
        b = next_bank()
        for c in range(kch):
            mm(psb(b)[:, 0:n], Wsb[:, c, o * 128:(o + 1) * 128], rhs_of(c), c == 0, c == kch - 1, reads + [R(Wsb)], b)
        return b

    def norm_stats(xsq_b, xsq_ap_of, n, r_b, ln_b):
        b = next_bank()
        for c in range(8):
            mm(psb(b)[:, 0:n], ones[:, :], xsq_ap_of(c), c == 0, c == 7, [R(xsq_b), R(ones)], b)
        rsqrt_act(r_b, r_b[:, 0:n], psb(b)[:, 0:n], [R(PS, b)], ln_b, ln_b[:, 0:n], 1.0 / D)

    def d1_tile(kind, r):
        halo = kind == "halo"
        P.barrier()
        if halo:
            n_u, col0 = NH17, NMAIN
        else:
            n_u, col0 = TW, TW * r
        dma(xt1, xt1[:, :, 0:n_u], xT_own, xT_own[:].rearrange("(c p) t -> p c t", p=128)[:, :, col0:col0 + n_u])
        tt("dve", xb1[:, :, 0:n_u], xt1[:, :, 0:n_u], gmix_bc.to_broadcast([128, 8, n_u]), ALU.mult,
           [R(xt1), R(vec)], [R(xb1)])
        act(xq1[:, :, 0:n_u], xt1[:, :, 0:n_u], AF.Square, [R(xt1)], [R(xq1)])
        norm_stats(xq1, lambda c: xq1[:, c, 0:n_u], n_u, r1, ln1)
        if halo:
            for g in range(4):
                b = linear(Wu, 8, g, lambda c: xb1[:, c, 0:n_u], n_u, [R(xb1)])
                tt("dve", uh[:, g, :], psb(b)[:, 0:n_u], r1[:, 0:n_u], ALU.mult, [R(PS, b), R(r1)], [R(uh, g)])
            n = NHQ
            uv = uh[:].rearrange("p g (m k) -> p g m k", k=17)
            sa = s1[:].rearrange("p g k -> p (g k)")[:, 0:4 * NH17].rearrange("p (g m k) -> p g m k", g=4, k=17)
            sb2 = s2[:].rearrange("p g k -> p (g k)")[:, 0:4 * NH17].rearrange("p (g m k) -> p g m k", g=4, k=17)
            width, lo = 17, 15
            sl = lambda X, g0, a, b_: X[:, g0:4, :, a:b_]
            sg1 = lambda X, g, a, b_: X[:, g, :, a:b_]
            u_b = uh
        else:
            for g in range(4):
                b = linear(Wu, 8, g, lambda c: xb1[:, c, 0:TW], TW, [R(xb1)])
                tt("dve", ub[:, g, 15:271], psb(b)[:, 0:TW], r1[:, 0:TW], ALU.mult, [R(PS, b), R(r1)], [R(ub, g)])
            cp("pool", ub[:, :, 0:15], uh[:].rearrange("p g (m k) -> p g m k", k=17)[:, :, r, 2:17],
               [R(uh)], [R(ub, "h")])
            n = TW
            uv, sa, sb2 = ub[:], s1[:], s2[:]
            width, lo = 271, 15
            sl = lambda X, g0, a, b_: X[:, g0:4, a:b_]
            sg1 = lambda X, g, a, b_: X[:, g, a:b_]
            u_b = ub
        cur, cur_b = uv, u_b
        other = [(sa, s1), (sb2, s2)]
        oi = 0
        for lvl, step in enumerate((1, 2, 4, 8)):
            dst, dst_b = other[oi]
            oi ^= 1
            tt("dve", sl(dst, lvl, step, width), sl(cur, lvl, step, width), sl(cur, lvl, 0, width - step),
               ALU.add, [R(cur_b)], [R(dst_b)])
            wv = float(2 ** (lvl + 1))
            ssrc = sg1(dst, lvl, lo, width)
            usrc = sg1(uv, lvl, lo, width)
            if halo:
                pdst = pooled[:, lvl, 0:n].rearrange("p (m k) -> p m k", k=2)
                if lvl == 3:
                    tt("dve", ssrc, ssrc, c16[:].rearrange("p (m k) -> p m k", k=2), ALU.mult,
                       [R(dst_b), R(c16)], [R(dst_b)])
            else:
                pdst = pooled[:, lvl, :]
            stt("dve", pdst, usrc, -wv, ssrc, ALU.mult, ALU.add, [R(u_b), R(dst_b)], [R(pooled, lvl)])
            cur, cur_b = dst, dst_b
        if stop == "D1a" or (stop == "M0a" and not halo):
            raise _Stop()
        shp = lambda ap: ap
        if halo:
            v17 = lambda ap: ap.rearrange("p c (m k) -> p c m k", k=17)[:, :, :, 15:17]
            v2 = lambda ap: ap.rearrange("p c (m k) -> p c m k", k=2)
            cp("dve", v2(xbq[:]), v17(xb1[:, :, 0:NH17]), [R(xb1)], [R(xbq)])
            cp("dve", v2(xtq[:]), v17(xt1[:, :, 0:NH17]), [R(xt1)], [R(xtq)])
            cp("dve", r1q[:].rearrange("p (m k) -> p m k", k=2),
               r1[:, 0:NH17].rearrange("p (m k) -> p m k", k=17)[:, :, 15:17], [R(r1)], [R(r1q)])
            xcol = lambda c: xbq[:, c, :]
            xcol_b = xbq
            r1v, r1v_b = r1q[:, :], r1q
            xres, xres_b = (lambda o: xtq[:, o, :]), xtq
            qcols = slice(NMAIN, NMAIN + NHQ)
            h1o, h1o_b = (lambda o: h1h[:, o, :]), h1h
        else:
            xcol = lambda c: xb1[:, c, 0:TW]
            xcol_b = xb1
            r1v, r1v_b = r1[:, 0:TW], r1
            xres, xres_b = (lambda o: xt1[:, o, 0:TW]), xt1
            qcols = slice(TW * r, TW * r + TW)
            h1o, h1o_b = (lambda o: xt1[:, o, 0:TW]), xt1
        for g in range(4):
            b = next_bank()
            mm(psb(b)[:, 0:n], Wgrp[:, g, :], pooled[:, g, 0:n], True, True, [R(pooled, g), R(Wgrp)], b)
            act(mixed[:, g, 0:n], psb(b)[:, 0:n], AF.Copy, [R(PS, b), R(pscw)], [R(mixed, g)], scale=pscw[:, g:g + 1])
        if stop == "D1b" or (stop == "M0b" and not halo):
            raise _Stop()
        for o in range(8):
            q = o % 2
            b = linear(Wga, 8, o, xcol, n, [R(xcol_b)])
            tt("dve", shp(tg[q][:, 0:n]), shp(psb(b)[:, 0:n]), r1v, ALU.mult, [R(PS, b), R(r1v_b)], [R(tg[q])])
            act(tg[q][:, 0:n], tg[q][:, 0:n], AF.Sigmoid, [R(tg[q])], [R(tg[q])])
            b = linear(Wab, 8, o, lambda c: attnT[:, c, qcols], n, [R(attnT)])
            tt("dve", m1[q][:, 0:n], psb(b)[:, 0:n], tg[q][:, 0:n], ALU.mult, [R(PS, b), R(tg[q])], [R(m1[q])])
            b = linear(Wgp, 8, o, xcol, n, [R(xcol_b)])
            tt("dve", shp(tg[q][:, 0:n]), shp(psb(b)[:, 0:n]), r1v, ALU.mult, [R(PS, b), R(r1v_b)], [R(tg[q])])
            act(tg[q][:, 0:n], tg[q][:, 0:n], AF.Sigmoid, [R(tg[q])], [R(tg[q])])
            b = linear(Wpb, 4, o, lambda c: mixed[:, c, 0:n], n, [R(mixed)])
            tt("dve", m2[q][:, 0:n], psb(b)[:, 0:n], tg[q][:, 0:n], ALU.mult, [R(PS, b), R(tg[q])], [R(m2[q])])
            tt("pool", merged[:, o, 0:n], m1[q][:, 0:n], m2[q][:, 0:n], ALU.add, [R(m1[q]), R(m2[q])], [R(merged, o)])
        if stop == "D1c" or (stop == "M0c" and not halo):
            raise _Stop()
        for o in range(8):
            b = linear(Wo, 8, o, lambda c: merged[:, c, 0:n], n, [R(merged)])
            tt("dve", shp(h1o(o)), shp(psb(b)[:, 0:n]), xres(o), ALU.add, [R(PS, b), R(xres_b)], [R(h1o_b, ("o", o))])
        src = h1h[:] if halo else xt1[:, :, 0:TW]
        dma(h1_scr, h1_scr[:].rearrange("(c p) t -> p c t", p=128)[:, :, qcols], h1o_b, src, no_waw=True)

    try:
        d1_tile("halo", 0)
        if stop == "D1d":
            raise _Stop()
        for r in range(ROWS):
            d1_tile("main", r)
            if stop == "M0d":
                raise _Stop()
    except _Stop:
        return finish()

    if stop == "D1":
        return finish()
    P.barrier()
    A.off = CONST_END
    Wup = wload("Wup", w_up, 0, 5632, 8)
    Wdn = A.alloc("Wdn", [128, 22, 1024], BF16)
    dma(Wdn, Wdn[:], w_down, w_down[:].rearrange("(c p) n -> p c n", p=128), eng="pool")
    uph = A.alloc("uph", [128, NCH_FF, NHQ], F32)
    h1b = A.alloc("h1b", [128, 8, TW], F32)
    hb = A.alloc("hb", [128, 8, TW], BF16)
    r2 = A.alloc("r2", [128, TW], F32)
    ln2 = A.alloc("ln2", [128, TW], F32)
    upv = [A.alloc("upv", [128, 258], F32) for _ in range(2)]
    upg = [A.alloc("upg", [128, 258], F32) for _ in range(2)]
    yv = [A.alloc("yv", [128, TW], F32) for _ in range(2)]
    yg = [A.alloc("yg", [128, TW], F32) for _ in range(2)]
    actT = A.alloc("actT", [128, 22, TW], BF16)
    h2 = A.alloc("h2", [128, 8, TW], F32)

    def d2_tile(kind, r):
        halo = kind == "halo"
        P.barrier()
        if halo:
            n, qcols = NHQ, slice(NMAIN, NMAIN + NHQ)
        else:
            n, qcols = TW, slice(TW * r, TW * r + TW)
        dma(h1b, h1b[:, :, 0:n], h1_scr, h1_scr[:].rearrange("(c p) t -> p c t", p=128)[:, :, qcols])
        act(hb[:, :, 0:n], h1b[:, :, 0:n], AF.Square, [R(h1b)], [R(hb)])
        norm_stats(hb, lambda c: hb[:, c, 0:n], n, r2, ln2)
        tt("dve", h2[:, :, 0:n], h1b[:, :, 0:n], r2[:, 0:n].unsqueeze(1).to_broadcast([128, 8, n]), ALU.mult,
           [R(h1b), R(r2)], [R(h2)])
        tt("pool", hb[:, :, 0:n], h2[:, :, 0:n], vec[:, G_FFN:G_FFN + 8].unsqueeze(2).to_broadcast([128, 8, n]),
           ALU.mult, [R(h2), R(vec)], [R(hb)])
        if halo:
            for c in range(NCH_FF):
                b = linear(Wup, 8, c, lambda k: hb[:, k, 0:n], n, [R(hb)])
                act(uph[:, c, :], psb(b)[:, 0:n], AF.Copy, [R(PS, b)], [R(uph, c)])
            return
        for c in range(22):
            q = c % 2
            for (cc, ubuf, ybuf) in ((c, upv[q], yv[q]), (22 + c, upg[q], yg[q])):
                b = linear(Wup, 8, cc, lambda k: hb[:, k, 0:TW], TW, [R(hb)])
                w0 = vec[:, CW + 3 * cc + 0:CW + 3 * cc + 1]
                w1 = vec[:, CW + 3 * cc + 1:CW + 3 * cc + 2]
                w2 = vec[:, CW + 3 * cc + 2:CW + 3 * cc + 3]
                bb = vec[:, CB + cc:CB + cc + 1]
                act(ubuf[:, 2:258], psb(b)[:, 0:TW], AF.Copy, [R(PS, b)], [R(ubuf, "m")])
                act(ybuf[:], psb(b)[:, 0:TW], AF.Identity, [R(PS, b), R(vec)], [R(ybuf)], scale=w2, bias=bb)
                cp("pool", ubuf[:, 0:2], uph[:, cc, 2 * r:2 * r + 2], [R(uph)], [R(ubuf, "h")])
                ur = [R(ubuf, "m"), R(ubuf, "h"), R(vec)]
                stt("dve", ybuf[:], ubuf[:, 1:257], w1, ybuf[:], ALU.mult, ALU.add, ur + [R(ybuf)], [R(ybuf)])
                stt("dve", ybuf[:], ubuf[:, 0:256], w0, ybuf[:], ALU.mult, ALU.add, ur + [R(ybuf)], [R(ybuf)])
            act(yg[q][:], yg[q][:], AF.Silu, [R(yg[q])], [R(yg[q])])
            tt("dve", actT[:, c, :], yg[q][:], yv[q][:], ALU.mult, [R(yg[q]), R(yv[q])], [R(actT, c)])
        for o in range(8):
            b = linear(Wdn, 22, o, lambda k: actT[:, k, :], TW, [R(actT)])
            tt("dve", h2[:, o, :], psb(b)[:, 0:TW], h1b[:, o, 0:TW], ALU.add, [R(PS, b), R(h1b)], [R(h2, o)])
        act(hb[:], h2[:], AF.Square, [R(h2)], [R(hb)])
        norm_stats(hb, lambda c: hb[:, c, :], TW, r2, ln2)
        tt("dve", h2[:], h2[:], r2[:, 0:TW].unsqueeze(1).to_broadcast([128, 8, TW]), ALU.mult,
           [R(h2), R(r2)], [R(h2)])
        tt("pool", h2[:], h2[:], vec[:, G_FIN:G_FIN + 8].unsqueeze(2).to_broadcast([128, 8, TW]), ALU.mult,
           [R(h2), R(vec)], [R(h2)])
        dma(yT, yT[:].rearrange("(c p) t -> p c t", p=128)[:, :, qcols], h2, h2[:], no_waw=True)

    d2_tile("halo", 0)
    for r in range(ROWS):
        d2_tile("main", r)

    return finish()


_CACHE = {}


def _rope_tables(npos):
    pos = np.arange(npos, dtype=np.float32)
    inv = (1.0 / (np.float32(10000.0) ** (np.arange(0, 64, 2, dtype=np.float32) / np.float32(64)))).astype(np.float32)
    ang = (pos[:, None] * inv[None, :]).astype(np.float32)
    return np.cos(ang).astype(np.float32), np.sin(ang).astype(np.float32)


def _core_layout(j):
    main_p = np.zeros(NMAIN, np.int64)
    h17_p = np.zeros(NH17, np.int64)
    for m in range(ROWS):
        qb = 4 * m + j
        main_p[m * BQ:(m + 1) * BQ] = 16 + BQ * qb + np.arange(BQ)
        h17_p[m * 17:(m + 1) * 17] = 16 + BQ * qb - 17 + np.arange(17)
    hq_p = h17_p.reshape(ROWS, 17)[:, 15:17].reshape(-1)
    return main_p, h17_p, hq_p


def _prep_shared(inputs):
    f = lambda a: np.ascontiguousarray(np.asarray(a, dtype=np.float32))
    cosf, sinf = _rope_tables(NPOS)
    vecs = np.zeros((128, 205), np.float32)
    vecs[:, 0:8] = f(inputs["g_mix"])[0].reshape(8, 128).T
    vecs[:, 8:16] = f(inputs["g_ffn"])[0].reshape(8, 128).T
    vecs[:, 16:24] = f(inputs["g_final"]).reshape(8, 128).T
    vecs[:, 24:28] = f(inputs["pool_scale"])[0].reshape(4, 128).T
    vecs[:, 28] = f(inputs["g_subln"])[0]
    cw = f(inputs["conv_w"])[0]
    vecs[:, 29:29 + 132] = cw.reshape(3, NCH_FF, 128).transpose(2, 1, 0).reshape(128, 132)
    vecs[:, 161:205] = f(inputs["conv_b"])[0].reshape(NCH_FF, 128).T
    shared = {
        "cosK": cosf, "sinK": sinf,
        "w_in": f(inputs["w_in"])[0], "lam": f(inputs["lam"])[0].reshape(1, 256), "vecs": vecs,
        "w_grp": f(inputs["w_pool_grp"])[0], "w_attn_br": f(inputs["w_attn_br"])[0],
        "w_pool_br": f(inputs["w_pool_br"])[0], "w_out": f(inputs["w_out"])[0],
        "w_up": f(inputs["w_up"])[0], "w_down": f(inputs["w_down"])[0],
    }
    return shared, cosf, sinf


def _prep_core(c, x, metaT, xT_b, shared, cosf, sinf):
    j = c % 4
    main_p, h17_p, hq_p = _core_layout(j)
    xT_all = xT_b
    own_p = np.concatenate([main_p, h17_p, hq_p])
    valid = own_p >= 0
    xT_own = np.zeros((D, NOWN), np.float32)
    xT_own[:, valid] = xT_all[:, own_p[valid]]
    qpos = np.concatenate([main_p, hq_p])
    k = np.arange(128)[:, None, None]
    i = np.arange(8)[None, :, None]
    q = np.arange(BQ)[None, None, :]
    maskM = ((128 * i + k) <= (BQ * j + q)).astype(np.float32).reshape(128, 8 * BQ)
    keyp = np.full((65, 128), 1 << 30, np.int64)
    keyp[0, :16] = np.arange(16)
    keyp[1:, :] = 16 + 128 * np.arange(64)[:, None] + np.arange(128)[None, :]
    maskH = (keyp.T[:, :, None] <= hq_p[None, None, :]).astype(np.float32).reshape(128, 65 * NHQ)
    cnt16 = np.ones((128, NHQ), np.float32)
    cnt16[:, :] = (16.0 / np.minimum(16, hq_p + 1).astype(np.float64)).astype(np.float32)[None, :]
    m = dict(shared)
    m.update({
        "xT_all": xT_all, "xT_own": xT_own,
        "cosQ": np.ascontiguousarray(cosf[qpos]), "sinQ": np.ascontiguousarray(sinf[qpos]),
        "maskM": maskM, "maskH": maskH, "cnt16": cnt16,
    })
    return m


def kernel(**inputs):
    dbg = bool(inputs.pop("_dbg", False))
    stop = inputs.pop("_stop", None)
    cores = inputs.pop("_cores", list(range(8)))
    x = np.asarray(inputs["x"], dtype=np.float32)
    meta = np.asarray(inputs["meta_tokens"], dtype=np.float32)
    key = ("prog", dbg, stop)
    if key not in _CACHE:
        _CACHE[key] = build_program(dbg, stop)[0]
    nc = _CACHE[key]
    shared, cosf, sinf = _prep_shared(inputs)
    xT = [np.ascontiguousarray(np.concatenate([meta, x[b]], axis=0).T) for b in range(2)]
    in_maps = [_prep_core(c, x, None, xT[c // 4], shared, cosf, sinf) for c in cores]
    res = run_bass_kernel_spmd(nc, in_maps, core_ids=list(range(len(cores))))
    out = np.zeros((2, SEQ, D), np.float32)
    for ci, c in enumerate(cores):
        b, j = c // 4, c % 4
        yT = np.asarray(res.results[ci]["yT"])
        for m in range(ROWS):
            qb = 4 * m + j
            out[b, BQ * qb:BQ * (qb + 1), :] = yT[:, m * BQ:(m + 1) * BQ].T
    if dbg:
        return out, res
    return out
```

```python
import contextlib
import numpy as np
import concourse.bass as bass
import concourse.mybir as mybir
from concourse.bass_utils import run_bass_kernel_spmd

F32 = mybir.dt.float32
BF16 = mybir.dt.bfloat16
AF = mybir.ActivationFunctionType
ALU = mybir.AluOpType

ENGS = ("pe", "act", "dve", "pool", "sp")

D = 1024
SEQ = 8192
NMETA = 16
NPOS = NMETA + SEQ
NH = 8
DFF = 2816
NCH_FF = 44
EPS = 1e-6
ROWS = 8
BQ = 256
NMAIN = ROWS * BQ
NH17 = ROWS * 17
NHQ = ROWS * 2
NOWN = NMAIN + NH17 + NHQ
NQ = NMAIN + NHQ
LAM_INIT = 0.2


class Buf:
    def __init__(self, t, name, exclusive=False):
        self.t = t
        self.name = name
        self.regions = {}
        self.exclusive = exclusive

    def __getitem__(self, idx):
        return self.t[idx]


class _Region:
    __slots__ = ("last_write", "reads", "sem", "ndma")

    def __init__(self):
        self.last_write = None
        self.reads = []
        self.sem = None
        self.ndma = 0


class _Op:
    __slots__ = ("eng", "fn", "deps", "is_dma", "signal", "region", "dma_idx", "sig_idx", "idx")


def R(buf, key=None):
    return (buf, key)


class Prog:
    def __init__(self, nc):
        self.nc = nc
        self.ops = []
        self.stack = contextlib.ExitStack()
        self.last_on_eng = {}
        self.barrier_deps = {}

    def dram(self, name, shape, dtype, kind="Internal"):
        t = self.nc.dram_tensor(name, list(shape), dtype, kind=kind)
        return Buf(t.ap(), name)

    def _regs(self, acc):
        buf, key = acc
        if key is None:
            out = [buf.regions.setdefault(None, _Region())]
            out += [r for k, r in buf.regions.items() if k is not None]
            return out
        out = [buf.regions.setdefault(key, _Region())]
        if None in buf.regions:
            out.append(buf.regions[None])
        return out

    def barrier(self):
        deps = set(self.last_on_eng.values())
        seen = set()
        for o in self.ops:
            if o.is_dma:
                seen.add(id(o.region))
        last_dma = {}
        for o in self.ops:
            if o.is_dma:
                last_dma[id(o.region)] = o.idx
        deps.update(last_dma.values())
        for e in ENGS:
            self.barrier_deps[e] = set(deps) | self.barrier_deps.get(e, set())

    def op(self, eng, fn, reads=(), writes=(), dma=False, no_waw=False):
        o = _Op()
        o.eng, o.fn, o.is_dma, o.signal = eng, fn, dma, False
        o.idx = len(self.ops)
        reads = list(reads)
        writes = list(writes)
        excl = [a for a in reads if a[0].exclusive]
        if excl:
            reads = [a for a in reads if not a[0].exclusive]
            writes = writes + [a for a in excl if a not in writes]
        deps = set()
        for acc in reads:
            for r in self._regs(acc):
                if r.last_write is not None:
                    deps.add(r.last_write)
        for acc in writes:
            for r in self._regs(acc):
                if r.last_write is not None and not no_waw:
                    deps.add(r.last_write)
                deps.update(r.reads)
        if eng in self.barrier_deps:
            deps.update(self.barrier_deps.pop(eng))
        deps.discard(o.idx)
        o.deps = deps
        o.region = None
        if dma:
            buf, key = writes[0]
            o.region = buf.regions.setdefault(key, _Region())
        for acc in reads:
            buf, key = acc
            rl = buf.regions.setdefault(key, _Region()).reads
            if not dma:
                for i_, prev in enumerate(rl):
                    po = self.ops[prev]
                    if (not po.is_dma) and po.eng == eng:
                        rl[i_] = o.idx
                        break
                else:
                    rl.append(o.idx)
            else:
                rl.append(o.idx)
        for acc in writes:
            buf, key = acc
            reg = buf.regions.setdefault(key, _Region())
            reg.reads = []
            reg.last_write = o.idx
            if key is None:
                for k, r in buf.regions.items():
                    if k is not None:
                        r.last_write = o.idx
                        r.reads = []
        self.ops.append(o)
        if not dma:
            self.last_on_eng[eng] = o.idx
        return o

    def emit(self):
        nc = self.nc
        ops = self.ops
        for o in ops:
            for d in o.deps:
                od = ops[d]
                if od.eng == "pe" and o.eng == "pe" and not od.is_dma:
                    continue
                od.signal = True
        cnt = {e: 0 for e in ENGS}
        for o in ops:
            if o.is_dma:
                o.region.ndma += 1
                o.dma_idx = o.region.ndma
            elif o.signal:
                cnt[o.eng] += 1
                o.sig_idx = cnt[o.eng]
        self.sig_counts = cnt
        sems = {e: self.stack.enter_context(nc.semaphore(f"s_{e}")) for e in ENGS}
        nreg = 0
        for o in ops:
            if o.is_dma and o.region.sem is None:
                nreg += 1
                o.region.sem = self.stack.enter_context(nc.semaphore(f"d_{nreg}"))
        self.n_dma_sems = nreg
        per_eng = {e: [o for o in ops if o.eng == e] for e in ENGS}

        def run(ename, eng):
            waited = {}
            for o in per_eng[ename]:
                for d in sorted(o.deps):
                    od = ops[d]
                    if od.is_dma:
                        key = ("d", id(od.region))
                        val = 16 * od.dma_idx
                        sem = od.region.sem
                    else:
                        if od.eng == "pe" and ename == "pe":
                            continue
                        key = ("e", od.eng)
                        val = od.sig_idx
                        sem = sems[od.eng]
                    if waited.get(key, 0) >= val:
                        continue
                    waited[key] = val
                    eng.wait_ge(sem, val)
                ins = o.fn(eng)
                if o.is_dma:
                    ins.then_inc(o.region.sem, 16)
                elif o.signal:
                    ins.then_inc(sems[ename], 1)
            return waited

        with nc.Block() as block:
            @block.tensor
            def _(e):
                run("pe", e)

            @block.scalar
            def _(e):
                run("act", e)

            @block.vector
            def _(e):
                run("dve", e)

            @block.gpsimd
            def _(e):
                run("pool", e)

            @block.sync
            def _(e):
                w = run("sp", e)
                seen = set()
                for o in ops:
                    if o.is_dma and id(o.region) not in seen:
                        seen.add(id(o.region))
                        val = 16 * o.region.ndma
                        if w.get(("d", id(o.region)), 0) < val:
                            e.wait_ge(o.region.sem, val)

    def close(self):
        self.stack.close()


def build_program(dbg=False, stop=None):
    nc = bass.Bass("TRN2", target_bir_lowering=False)
    P = Prog(nc)
    IN = "ExternalInput"

    class _Stop(Exception):
        pass

    def finish():
        P.emit()
        P.close()
        return nc, P

    xT_all = P.dram("xT_all", [D, NMETA + SEQ], F32, IN)
    xT_own = P.dram("xT_own", [D, NOWN], F32, IN)
    cosK = P.dram("cosK", [NPOS, 32], F32, IN)
    sinK = P.dram("sinK", [NPOS, 32], F32, IN)
    cosQ = P.dram("cosQ", [NQ, 32], F32, IN)
    sinQ = P.dram("sinQ", [NQ, 32], F32, IN)
    maskM = P.dram("maskM", [128, 8 * BQ], F32, IN)
    maskH = P.dram("maskH", [128, 65 * NHQ], F32, IN)
    cnt16 = P.dram("cnt16", [128, NHQ], F32, IN)
    w_in = P.dram("w_in", [D, 5632], F32, IN)
    lam = P.dram("lam", [1, 256], F32, IN)
    vecs = P.dram("vecs", [128, 8 + 8 + 8 + 4 + 1 + 44 * 3 + 44], F32, IN)
    w_grp = P.dram("w_grp", [4, 128, 128], F32, IN)
    w_attn_br = P.dram("w_attn_br", [D, D], F32, IN)
    w_pool_br = P.dram("w_pool_br", [512, D], F32, IN)
    w_out = P.dram("w_out", [D, D], F32, IN)
    w_up = P.dram("w_up", [D, 5632], F32, IN)
    w_down = P.dram("w_down", [DFF, D], F32, IN)
    yT = P.dram("yT", [D, NMAIN], F32, "ExternalOutput")
    kT_scr = P.dram("kT_scr", [NH, 128, NPOS], BF16)
    v_scr = P.dram("v_scr", [NH, 128, 64, 128], BF16)
    vm_scr = P.dram("vm_scr", [NH, 16, 128], BF16)
    h1_scr = P.dram("h1_scr", [D, NQ], F32, "ExternalOutput" if dbg else "Internal")
    dbgT = {}
    if dbg:
        dbgT["QT"] = P.dram("dbg_QT", [128, NH * NQ], BF16, "ExternalOutput")
        dbgT["attnT"] = P.dram("dbg_attnT", [128, NH * NQ], BF16, "ExternalOutput")
        dbgT["kT"] = P.dram("dbg_kT", [128, NPOS], BF16, "ExternalOutput")
        dbgT["v"] = P.dram("dbg_v", [128, 64 * 128], BF16, "ExternalOutput")

    ARENA_F32 = 51200
    arena_t = P.stack.enter_context(nc.sbuf_tensor("arena", [128, ARENA_F32], F32))
    psum_t = P.stack.enter_context(nc.psum_tensor("psum", [128, 4096], F32))
    PS = Buf(psum_t, "psum", exclusive=True)

    class Arena:
        def __init__(self):
            self.off = 0
            self.n = 0

        def alloc(self, name, shape, dtype):
            esz = 4 if dtype == F32 else 2
            nfree = int(np.prod(shape[1:]))
            nbytes = (nfree * esz + 31) // 32 * 32
            o4 = self.off // 4
            n4 = nbytes // 4
            assert o4 + n4 <= ARENA_F32, f"arena overflow at {name}: {self.off + nbytes}"
            ap = arena_t[:, o4:o4 + n4]
            if dtype != F32:
                ap = ap.bitcast(dtype)
            ap = ap[:, 0:nfree]
            if len(shape) > 2:
                names = " ".join(f"d{i}" for i in range(len(shape) - 1))
                kw = {f"d{i}": shape[i + 1] for i in range(len(shape) - 1)}
                ap = ap.rearrange(f"p ({names}) -> p {names}", **kw)
            self.off += nbytes
            self.n += 1
            return Buf(ap, f"{name}_{self.n}")

    A = Arena()

    def psb(bank, ncol=512, col0=0, dtype=F32):
        ap = psum_t[:, bank * 512 + col0: bank * 512 + col0 + ncol]
        return ap

    bank_rr = [0]

    def next_bank(lo=0, hi=8):
        b = lo + bank_rr[0] % (hi - lo)
        bank_rr[0] += 1
        return b

    def dma(out_b, out_ap, in_b, in_ap, eng="sp", okey=None, ikey=None, no_waw=False):
        P.op(eng, lambda e: e.dma_start(out=out_ap, in_=in_ap), reads=[R(in_b, ikey)],
             writes=[R(out_b, okey)], dma=True, no_waw=no_waw)

    def mm(out_ap, lhsT, rhs, start, stop, reads, bank):
        P.op("pe", lambda e: e.matmul(out_ap, lhsT=lhsT, rhs=rhs, start=start, stop=stop),
             reads=reads, writes=[R(PS, bank)])

    def act(out_ap, in_ap, func, reads, writes, scale=1.0, bias=None):
        if bias is None:
            P.op("act", lambda e: e.activation(out=out_ap, in_=in_ap, func=func, scale=scale),
                 reads=reads, writes=writes)
        else:
            P.op("act", lambda e: e.activation(out=out_ap, in_=in_ap, func=func, scale=scale, bias=bias),
                 reads=reads, writes=writes)

    def tt(eng, out_ap, in0, in1, op, reads, writes):
        P.op(eng, lambda e: e.tensor_tensor(out=out_ap, in0=in0, in1=in1, op=op), reads=reads, writes=writes)

    def ts(eng, out_ap, in0, s1, s2, op0, op1, reads, writes):
        if op1 is None:
            P.op(eng, lambda e: e.tensor_scalar(out=out_ap, in0=in0, scalar1=s1, scalar2=None, op0=op0),
                 reads=reads, writes=writes)
        else:
            P.op(eng, lambda e: e.tensor_scalar(out=out_ap, in0=in0, scalar1=s1, scalar2=s2, op0=op0, op1=op1),
                 reads=reads, writes=writes)

    def stt(eng, out_ap, in0, scalar, in1, op0, op1, reads, writes):
        P.op(eng, lambda e: e.scalar_tensor_tensor(out=out_ap, in0=in0, scalar=scalar, in1=in1, op0=op0, op1=op1),
             reads=reads, writes=writes)

    def tr(out_ap, in_ap, id_ap, reads, bank):
        P.op("pe", lambda e: e.transpose(out_ap, in_ap, id_ap), reads=reads, writes=[R(PS, bank)])

    def cp(eng, out_ap, in_ap, reads, writes):
        P.op(eng, lambda e: e.tensor_copy(out=out_ap, in_=in_ap), reads=reads, writes=writes)

    def rsqrt_act(out_b, out_ap, in_ap, in_reads, tmp_b, tmp_ap, inv_n):
        act(tmp_ap, in_ap, AF.Ln, in_reads, [R(tmp_b)], scale=inv_n, bias=EPS)
        act(out_ap, tmp_ap, AF.Exp, [R(tmp_b)], [R(out_b)], scale=-0.5)

    ident = A.alloc("ident", [128, 128], BF16)
    ones = A.alloc("ones", [128, 128], BF16)
    idf = A.alloc("idf", [128, 128], F32)
    vec = A.alloc("vec", [128, 205], F32)
    lamb = A.alloc("lamb", [128, 256], F32)
    lprod = A.alloc("lprod", [128, 128], F32)
    lsum = A.alloc("lsum", [128, 2], F32)
    neglam = A.alloc("neglam", [128, 1], F32)
    gsub08 = A.alloc("gsub08", [128, 1], F32)
    pscw = A.alloc("pscw", [128, 4], F32)
    mM = A.alloc("mM", [128, 8, BQ], BF16)
    mH = A.alloc("mH", [128, 65, NHQ], BF16)
    c16 = A.alloc("c16", [128, NHQ], F32)
    G_MIX, G_FFN, G_FIN, PSC, GSUB, CW, CB = 0, 8, 16, 24, 28, 29, 29 + 132

    P.op("pool", lambda e: e.memset(idf[:], 1.0), writes=[R(idf)])
    P.op("pool", lambda e: e.affine_select(out=idf[:], in_=idf[:], pattern=[[-1, 128]], compare_op=ALU.is_equal,
                                           fill=0.0, base=0, channel_multiplier=1), reads=[R(idf)], writes=[R(idf)])
    cp("dve", ident[:], idf[:], [R(idf)], [R(ident)])
    P.op("dve", lambda e: e.memset(ones[:], 1.0), writes=[R(ones)])
    dma(vec, vec[:], vecs, vecs[:])
    dma(lamb, lamb[:], lam, lam[:].to_broadcast([128, 256]))
    dma(c16, c16[:], cnt16, cnt16[:])
    dma(mM, mM[:], maskM, maskM[:].rearrange("p (i q) -> p i q", i=8), eng="pool")
    dma(mH, mH[:], maskH, maskH[:].rearrange("p (b q) -> p b q", b=65), eng="pool")
    lv = lamb[:].rearrange("p (a b d) -> p a b d", a=2, b=2)
    tt("dve", lprod[:].rearrange("p (a d) -> p a d", a=2), lv[:, :, 0, :], lv[:, :, 1, :], ALU.mult,
       [R(lamb)], [R(lprod)])
    P.op("dve", lambda e: e.reduce_sum(out=lsum[:], in_=lprod[:].rearrange("p (a d) -> p a d", a=2),
                                       axis=mybir.AxisListType.X), reads=[R(lprod)], writes=[R(lsum)])
    act(lsum[:], lsum[:], AF.Exp, [R(lsum)], [R(lsum)])
    stt("dve", neglam[:], lsum[:, 1:2], -LAM_INIT, lsum[:, 0:1], ALU.add, ALU.subtract, [R(lsum)], [R(neglam)])
    ts("dve", gsub08[:], vec[:, GSUB:GSUB + 1], 1.0 - LAM_INIT, None, ALU.mult, None, [R(vec)], [R(gsub08)])
    for g in range(4):
        ts("dve", pscw[:, g:g + 1], vec[:, PSC + g:PSC + g + 1], 1.0 / (2 ** (g + 1)), None, ALU.mult, None,
           [R(vec)], [R(pscw, g)])
    CONST_END = A.off

    QT = A.alloc("QT", [128, NH, NQ], BF16)
    QT_END = A.off
    attnT = A.alloc("attnT", [128, NH, NQ], BF16)
    ATT_END = A.off

    def w_view(wb, lo, hi):
        return wb[:].rearrange("(c p) n -> p c n", p=128)[:, :, lo:hi]

    def proj_phase(tiles, W_list, sink):
        xt = [A.alloc("xt", [128, 8, 512], F32) for _ in range(2)]
        xb = [A.alloc("xb", [128, 8, 512], BF16) for _ in range(2)]
        xsq1 = A.alloc("xsq", [128, 8, 512], BF16)
        xsq = [xsq1, xsq1]
        ct = [A.alloc("ct", [128, 4, 32], F32) for _ in range(2)]
        st = [A.alloc("st", [128, 4, 32], F32) for _ in range(2)]
        r4 = [A.alloc("r4", [128, 4], F32) for _ in range(2)]
        lnr = [A.alloc("lnr", [128, 4], F32) for _ in range(2)]
        kf = [A.alloc("kf", [128, 1024], F32) for _ in range(2)]
        tA1 = A.alloc("tA", [128, 1024], F32)
        tB1 = A.alloc("tB", [128, 1024], F32)
        tA = [tA1, tA1]
        tB = [tB1, tB1]
        kb16 = [A.alloc("kb16", [128, 1024], BF16) for _ in range(2)]
        gmix_bc = vec[:, G_MIX:G_MIX + 8].unsqueeze(2)
        bigs = [(0, 1), (2, 3), (4, 5)]
        big_i = [0]
        RB, TB = 6, 7
        bcount = [0]

        def load(ti):
            src, col0, ncols, blocks, cosd, sind, row0 = tiles[ti]
            s = ti % 2
            dma(xt[s], xt[s][:, :, 0:ncols], src,
                src[:].rearrange("(c p) t -> p c t", p=128)[:, :, col0:col0 + ncols])
            for bi, (boff, nb) in enumerate(blocks):
                dma(ct[s], ct[s][0:nb, bi, :], cosd, cosd[row0 + boff:row0 + boff + nb, :], okey=bi)
                dma(st[s], st[s][0:nb, bi, :], sind, sind[row0 + boff:row0 + boff + nb, :], okey=bi)

        pending = []

        def flush_one():
            ti_, wi_, bi_, boff_, nb_, q_ = pending.pop(0)
            tps = psb(TB).bitcast(BF16)
            for h in range(NH):
                tr(tps[:, h * 128:h * 128 + nb_], kb16[q_][0:nb_, h * 128:(h + 1) * 128], ident[0:nb_, 0:nb_],
                   [R(kb16[q_], 0), R(kb16[q_], 1), R(ident)], TB)
            tview = tps[:, 0:1024].rearrange("p (h t) -> p h t", h=NH)[:, :, 0:nb_]
            sink(ti_, wi_, bi_, boff_, nb_, (tview, [R(PS, TB)], None, None))

        load(0)
        for ti in range(len(tiles)):
            src, col0, ncols, blocks, cosd, sind, row0 = tiles[ti]
            s = ti % 2
            if ti + 1 < len(tiles):
                load(ti + 1)
            tt("dve", xb[s][:, :, 0:ncols], xt[s][:, :, 0:ncols], gmix_bc.to_broadcast([128, 8, ncols]), ALU.mult,
               [R(xt[s]), R(vec)], [R(xb[s])])
            act(xsq[s][:, :, 0:ncols], xt[s][:, :, 0:ncols], AF.Square, [R(xt[s])], [R(xsq[s])])
            for bi, (boff, nb) in enumerate(blocks):
                for c in range(8):
                    mm(psb(RB)[0:nb, bi:bi + 1], xsq[s][:, c, boff:boff + nb], ones[:, 0:1], c == 0, c == 7,
                       [R(xsq[s]), R(ones)], RB)
            nbl = len(blocks)
            nb0 = blocks[0][1]
            act(lnr[s][0:nb0, 0:nbl], psb(RB)[0:nb0, 0:nbl], AF.Ln, [R(PS, RB)], [R(lnr[s])], scale=1.0 / D, bias=EPS)
            act(r4[s][0:nb0, 0:nbl], lnr[s][0:nb0, 0:nbl], AF.Exp, [R(lnr[s])], [R(r4[s])], scale=-0.5)
            for bi, (boff, nb) in enumerate(blocks):
                for wi, (Wsb, kind) in enumerate(W_list):
                    bk = bigs[big_i[0] % 3]
                    big_i[0] += 1
                    for half in range(2):
                        for c in range(8):
                            mm(psb(bk[half])[0:nb, :], xb[s][:, c, boff:boff + nb],
                               Wsb[:, c, half * 512:(half + 1) * 512], c == 0, c == 7,
                               [R(xb[s]), R(Wsb)], bk[half])
                    big_ap = psum_t[0:nb, bk[0] * 512:bk[0] * 512 + 1024]
                    q = bcount[0] % 2
                    if kind == "rope":
                        bcount[0] += 1
                    if kind == "plain":
                        sink(ti, wi, bi, boff, nb, (big_ap, [R(PS, bk[0]), R(PS, bk[1])], r4[s][0:nb, bi:bi + 1], R(r4[s])))
                        continue
                    act(kf[q][0:nb, :], big_ap, AF.Copy, [R(PS, bk[0]), R(PS, bk[1]), R(r4[s])], [R(kf[q])],
                        scale=r4[s][0:nb, bi:bi + 1])
                    kv = kf[q][0:nb, :].rearrange("p (a b d) -> p a b d", a=16, b=2)
                    av = tA[q][0:nb, :].rearrange("p (a b d) -> p a b d", a=16, b=2)
                    bv = tB[q][0:nb, :].rearrange("p (a b d) -> p a b d", a=16, b=2)
                    ov = kb16[q][0:nb, :].rearrange("p (a b d) -> p a b d", a=16, b=2)
                    cosb = ct[s][0:nb, bi, :]
                    sinb = st[s][0:nb, bi, :]
                    tt("pool", av, kv, cosb.unsqueeze(1).unsqueeze(1).to_broadcast([nb, 16, 2, 32]), ALU.mult,
                       [R(kf[q]), R(ct[s], bi)], [R(tA[q])])
                    sb_ = sinb.unsqueeze(1).to_broadcast([nb, 16, 32])
                    tt("dve", bv[:, :, 0, :], kv[:, :, 1, :], sb_, ALU.mult, [R(kf[q]), R(st[s], bi)], [R(tB[q], 0)])
                    tt("dve", bv[:, :, 1, :], kv[:, :, 0, :], sb_, ALU.mult, [R(kf[q]), R(st[s], bi)], [R(tB[q], 1)])
                    tt("dve", ov[:, :, 0, :], av[:, :, 0, :], bv[:, :, 0, :], ALU.subtract,
                       [R(tA[q]), R(tB[q], 0)], [R(kb16[q], 0)])
                    tt("dve", ov[:, :, 1, :], av[:, :, 1, :], bv[:, :, 1, :], ALU.add,
                       [R(tA[q]), R(tB[q], 1)], [R(kb16[q], 1)])
                    pending.append((ti, wi, bi, boff, nb, q))
                while len(pending) > 1:
                    flush_one()
        while pending:
            flush_one()

    A.off = QT_END
    Wk = A.alloc("Wk", [128, 8, 1024], BF16)
    Wv = A.alloc("Wv", [128, 8, 1024], BF16)
    dma(Wk, Wk[:], w_in, w_view(w_in, 1024, 2048), eng="pool")
    dma(Wv, Wv[:], w_in, w_view(w_in, 2048, 3072), eng="pool")
    kst = [A.alloc("kst", [128, NH, 512], BF16) for _ in range(2)]
    vst = [A.alloc("vst", [128, 4, 1024], BF16) for _ in range(2)]
    tilesA = [(xT_all, 0, 16, [(0, 16)], cosK, sinK, 0)]
    for t in range(16):
        tilesA.append((xT_all, 16 + 512 * t, 512, [(128 * b, 128) for b in range(4)], cosK, sinK, 16 + 512 * t))

    def sinkA(ti, wi, bi, boff, nb, data):
        s = ti % 2
        src_ap, src_reads, rcol, rread = data
        if wi == 0:
            act(kst[s][:, :, boff:boff + nb], src_ap, AF.Copy, src_reads, [R(kst[s], bi)])
            last = (bi == len(tilesA[ti][3]) - 1)
            if last:
                ncols = tilesA[ti][2]
                col0 = tilesA[ti][1]
                dma(kT_scr, kT_scr[:].rearrange("h f t -> f h t")[:, :, col0:col0 + ncols],
                    kst[s], kst[s][:, :, 0:ncols], no_waw=True)
        else:
            act(vst[s][0:nb, bi, :], src_ap, AF.Copy, src_reads + [rread], [R(vst[s], bi)], scale=rcol)
            if ti == 0:
                dma(vm_scr, vm_scr[:].rearrange("h t e -> t h e"),
                    vst[s], vst[s][0:16, 0, :].rearrange("t (h e) -> t h e", h=NH), ikey=bi)
            else:
                kb = 4 * (ti - 1) + bi
                dma(v_scr, v_scr[:, :, kb, :].rearrange("h t e -> t h e"),
                    vst[s], vst[s][:, bi, :].rearrange("t (h e) -> t h e", h=NH), ikey=bi, no_waw=True)

    proj_phase(tilesA, [(Wk, "rope"), (Wv, "plain")], sinkA)
    if stop == "A":
        if dbg:
            KTd = A.alloc("KTd", [128, NPOS], BF16)
            dma(KTd, KTd[:], kT_scr, kT_scr[0])
            dma(dbgT["kT"], dbgT["kT"][:], KTd, KTd[:])
        return finish()

    P.barrier()
    A.off = QT_END
    Wq = A.alloc("Wq", [128, 8, 1024], BF16)
    dma(Wq, Wq[:], w_in, w_view(w_in, 0, 1024), eng="pool")
    tilesB = []
    for t in range(4):
        tilesB.append((xT_own, 512 * t, 512, [(128 * b, 128) for b in range(4)], cosQ, sinQ, 512 * t))
    tilesB.append((xT_own, NMAIN + NH17, NHQ, [(0, NHQ)], cosQ, sinQ, NMAIN))

    def sinkB(ti, wi, bi, boff, nb, data):
        src_ap, src_reads, _, _ = data
        q0 = (512 * ti if ti < 4 else NMAIN) + boff
        act(QT[:, :, q0:q0 + nb], src_ap, AF.Copy, src_reads, [R(QT, (ti, bi))])

    proj_phase(tilesB, [(Wq, "rope")], sinkB)
    if dbg:
        dma(dbgT["QT"], dbgT["QT"][:], QT, QT[:].rearrange("p h q -> p (h q)"))

    if stop == "B":
        return finish()
    P.barrier()
    A.off = ATT_END
    KT = [A.alloc("KT", [128, NPOS], BF16) for _ in range(2)]
    VV = [A.alloc("VV", [128, 64, 128], BF16) for _ in range(2)]
    VM = [A.alloc("VM", [128, 128], BF16) for _ in range(2)]
    Pt = [A.alloc("Pt", [128, 2, 512], BF16) for _ in range(3)]
    Osb = A.alloc("Osb", [128, 512], F32)
    Lsb = A.alloc("Lsb", [128, 512], F32)
    t0b = A.alloc("t0b", [128, 256], F32)
    t1b = A.alloc("t1b", [128, 256], F32)
    ob = A.alloc("ob", [128, 256], F32)
    osq = A.alloc("osq", [128, 256], BF16)
    lnb = A.alloc("lnb", [128, 256], F32)
    rsb = A.alloc("rsb", [128, 256], F32)
    SBK = [(0, 1), (2, 3), (4, 5)]
    OB, LB, EB = 6, 7, 4
    NSB = 3

    def load_head(h):
        s = h % 2
        dma(KT[s], KT[s][:], kT_scr, kT_scr[h])
        dma(VV[s], VV[s][:], v_scr, v_scr[h])
        dma(VM[s], VM[s][0:16, :], vm_scr, vm_scr[h])

    def attention_tile(h, q0, nq, nkb, kps, mask_of):
        s = h % 2
        kt, vv, vm = KT[s], VV[s], VM[s]
        q_r = [R(QT)]
        steps = [(-1, 1)] + [(kb0, kps) for kb0 in range(0, nkb, kps)]
        nst = len(steps)
        w = kps * nq

        def qk(si):
            kb0, n = steps[si]
            sb = SBK[si % NSB]
            for kk in range(n):
                if kb0 < 0:
                    kcols, np_ = slice(0, 16), 16
                else:
                    c0 = 16 + 128 * (kb0 + kk)
                    kcols, np_ = slice(c0, c0 + 128), 128
                for comp in range(2):
                    mm(psb(sb[comp])[0:np_, kk * nq:(kk + 1) * nq], kt[comp * 64:(comp + 1) * 64, kcols],
                       QT[comp * 64:(comp + 1) * 64, h, q0:q0 + nq], True, True, q_r + [R(kt)], sb[comp])

        def expo(si):
            kb0, n = steps[si]
            sb = SBK[si % NSB]
            np_ = 16 if kb0 < 0 else 128
            pin = psum_t[0:np_, sb[0] * 512:sb[0] * 512 + 1024].rearrange("p (c w) -> p c w", c=2)[:, :, 0:n * nq]
            pt = Pt[si % NSB]
            act(pt[0:np_, :, 0:n * nq], pin, AF.Exp, [R(PS, sb[0]), R(PS, sb[1])], [R(pt)], scale=0.125)
            ptv = pt[0:np_, :, 0:n * nq].rearrange("p c (k q) -> p c k q", k=n)
            if kb0 < 0:
                m = mask_of(-1)
                if m is not None:
                    tt("dve", ptv[:, :, 0, :], ptv[:, :, 0, :], m[0].unsqueeze(1).to_broadcast([np_, 2, nq]), ALU.mult,
                       [R(pt), m[1]], [R(pt)])
            else:
                ms = [mask_of(kb0 + kk) for kk in range(n)]
                if all(m is not None for m in ms) and n > 1 and ms[0][2] is not None:
                    mb, mr, (mbuf, b0) = ms[0]
                    mall = mbuf[:, b0:b0 + n, :]
                    tt("dve", ptv, ptv, mall.unsqueeze(1).to_broadcast([np_, 2, n, nq]), ALU.mult,
                       [R(pt), mr], [R(pt)])
                else:
                    for kk, m in enumerate(ms):
                        if m is not None:
                            tt("dve", ptv[:, :, kk, :], ptv[:, :, kk, :],
                               m[0].unsqueeze(1).to_broadcast([np_, 2, nq]), ALU.mult, [R(pt), m[1]], [R(pt)])

        def pv(si):
            kb0, n = steps[si]
            pt = Pt[si % NSB]
            np_ = 16 if kb0 < 0 else 128
            ptv = pt[0:np_, :, 0:n * nq].rearrange("p c (k q) -> p c k q", k=n)
            for kk in range(n):
                first = (si == 0 and kk == 0)
                last = (si == nst - 1 and kk == n - 1)
                lhs_v = vm[0:16, :] if kb0 < 0 else vv[:, kb0 + kk, :]
                mm(psb(OB)[:, 0:2 * nq], lhs_v, ptv[:, :, kk, :], first, last, [R(pt), R(vm if kb0 < 0 else vv)], OB)
                mm(psb(LB)[:, 0:2 * nq], ones[0:np_, :], ptv[:, :, kk, :], first, last, [R(pt), R(ones)], LB)

        for si in range(min(2, nst)):
            qk(si)
        for si in range(nst):
            if si + 2 < nst:
                qk(si + 2)
            expo(si)
            pv(si)
        n2 = 2 * nq
        cp("dve", Lsb[:, 0:n2], psb(LB)[:, 0:n2], [R(PS, LB)], [R(Lsb)])
        cp("dve", Osb[:, 0:n2], psb(OB)[:, 0:n2], [R(PS, OB)], [R(Osb)])
        P.op("dve", lambda e: e.reciprocal(out=Lsb[:, 0:n2], in_=Lsb[:, 0:n2]), reads=[R(Lsb)], writes=[R(Lsb)])
        tt("dve", t0b[:, 0:nq], Osb[:, 0:nq], Lsb[:, 0:nq], ALU.mult, [R(Osb), R(Lsb)], [R(t0b)])
        tt("pool", t1b[:, 0:nq], Osb[:, nq:n2], Lsb[:, nq:n2], ALU.mult, [R(Osb), R(Lsb)], [R(t1b)])
        stt("dve", ob[:, 0:nq], t1b[:, 0:nq], neglam[:, 0:1], t0b[:, 0:nq], ALU.mult, ALU.add,
            [R(t1b), R(t0b), R(neglam)], [R(ob)])
        tt("pool", osq[:, 0:nq], ob[:, 0:nq], ob[:, 0:nq], ALU.mult, [R(ob)], [R(osq)])
        mm(psb(EB)[:, 0:nq], ones[:, :], osq[:, 0:nq], True, True, [R(osq), R(ones)], EB)
        rsqrt_act(rsb, rsb[:, 0:nq], psb(EB)[:, 0:nq], [R(PS, EB)], lnb, lnb[:, 0:nq], 1.0 / 128)
        stt("dve", attnT[:, h, q0:q0 + nq], ob[:, 0:nq], gsub08[:, 0:1], rsb[:, 0:nq], ALU.mult, ALU.mult,
            [R(ob), R(gsub08), R(rsb)], [R(attnT, (h, q0))])

    load_head(0)
    for h in range(NH):
        if h + 1 < NH:
            load_head(h + 1)
        for m in range(ROWS):
            nkb = 8 * m + 8

            def mask_main(blk, m=m):
                if blk < 0 or blk < 8 * m:
                    return None
                i = blk - 8 * m
                return (mM[:, i, :], R(mM), (mM, i))
            attention_tile(h, m * BQ, BQ, nkb, 2, mask_main)

        def mask_halo(blk):
            if blk < 0:
                return (mH[0:16, 0, :], R(mH), None)
            return (mH[:, 1 + blk, :], R(mH), (mH, 1 + blk))
        attention_tile(h, NMAIN, NHQ, 64, 16, mask_halo)
        if dbg and h == 0:
            dma(dbgT["kT"], dbgT["kT"][:], KT[0], KT[0][:])
            dma(dbgT["v"], dbgT["v"][:], VV[0], VV[0][:].rearrange("p b e -> p (b e)"))
    if dbg:
        dma(dbgT["attnT"], dbgT["attnT"][:], attnT, attnT[:].rearrange("p h q -> p (h q)"))

    if stop == "C":
        return finish()
    P.barrier()
    TW = 256

    def wload(name, wb, lo, hi, kch):
        Wsb = A.alloc(name, [128, kch, hi - lo], BF16)
        dma(Wsb, Wsb[:], wb, w_view(wb, lo, hi), eng="pool")
        return Wsb

    A.off = CONST_END
    Wu = wload("Wu", w_in, 3072, 3584, 8)
    Wgrp = A.alloc("Wgrp", [128, 4, 128], BF16)
    dma(Wgrp, Wgrp[:], w_grp, w_grp[:].rearrange("g c d -> c g d"), eng="pool")
    Wpb = wload("Wpb", w_pool_br, 0, 1024, 4)
    uh = A.alloc("uh", [128, 4, NH17], F32)
    h1h = A.alloc("h1h", [128, 8, NHQ], F32)
    xbq = A.alloc("xbq", [128, 8, NHQ], BF16)
    xtq = A.alloc("xtq", [128, 8, NHQ], F32)
    r1q = A.alloc("r1q", [128, NHQ], F32)
    assert A.off <= QT_END
    A.off = ATT_END
    Wga = wload("Wga", w_in, 3584, 4608, 8)
    Wgp = wload("Wgp", w_in, 4608, 5632, 8)
    Wab = wload("Wab", w_attn_br, 0, 1024, 8)
    Wo = wload("Wo", w_out, 0, 1024, 8)
    xt1 = A.alloc("xt1", [128, 8, TW], F32)
    xb1 = A.alloc("xb1", [128, 8, TW], BF16)
    xq1 = A.alloc("xq1", [128, 8, TW], BF16)
    merged = xq1
    r1 = A.alloc("r1", [128, TW], F32)
    ln1 = A.alloc("ln1", [128, TW], F32)
    ub = A.alloc("ub", [128, 4, 271], F32)
    s1 = A.alloc("s1", [128, 4, 271], F32)
    s2 = A.alloc("s2", [128, 4, 271], F32)
    pooled = A.alloc("pooled", [128, 4, TW], BF16)
    mixed = A.alloc("mixed", [128, 4, TW], BF16)
    tg = [A.alloc("tg", [128, TW], F32) for _ in range(2)]
    m1 = [A.alloc("m1", [128, TW], F32) for _ in range(2)]
    m2 = [A.alloc("m2", [128, TW], F32) for _ in range(2)]
    gmix_bc = vec[:, G_MIX:G_MIX + 8].unsqueeze(2)

    def linear(Wsb, kch, o, rhs_of, n, reads):
        b = next_bank()
        for c in range(kch):
            mm(psb(b)[:, 0:n], Wsb[:, c, o * 128:(o + 1) * 128], rhs_of(c), c == 0, c == kch - 1, reads + [R(Wsb)], b)
        return b

    def norm_stats(xsq_b, xsq_ap_of, n, r_b, ln_b):
        b = next_bank()
        for c in range(8):
            mm(psb(b)[:, 0:n], ones[:, :], xsq_ap_of(c), c == 0, c == 7, [R(xsq_b), R(ones)], b)
        rsqrt_act(r_b, r_b[:, 0:n], psb(b)[:, 0:n], [R(PS, b)], ln_b, ln_b[:, 0:n], 1.0 / D)

    def d1_tile(kind, r):
        halo = kind == "halo"
        P.barrier()
        if halo:
            n_u, col0 = NH17, NMAIN
        else:
            n_u, col0 = TW, TW * r
        dma(xt1, xt1[:, :, 0:n_u], xT_own, xT_own[:].rearrange("(c p) t -> p c t", p=128)[:, :, col0:col0 + n_u])
        tt("dve", xb1[:, :, 0:n_u], xt1[:, :, 0:n_u], gmix_bc.to_broadcast([128, 8, n_u]), ALU.mult,
           [R(xt1), R(vec)], [R(xb1)])
        act(xq1[:, :, 0:n_u], xt1[:, :, 0:n_u], AF.Square, [R(xt1)], [R(xq1)])
        norm_stats(xq1, lambda c: xq1[:, c, 0:n_u], n_u, r1, ln1)
        if halo:
            for g in range(4):
                b = linear(Wu, 8, g, lambda c: xb1[:, c, 0:n_u], n_u, [R(xb1)])
                tt("dve", uh[:, g, :], psb(b)[:, 0:n_u], r1[:, 0:n_u], ALU.mult, [R(PS, b), R(r1)], [R(uh, g)])
            n = NHQ
            uv = uh[:].rearrange("p g (m k) -> p g m k", k=17)
            sa = s1[:].rearrange("p g k -> p (g k)")[:, 0:4 * NH17].rearrange("p (g m k) -> p g m k", g=4, k=17)
            sb2 = s2[:].rearrange("p g k -> p (g k)")[:, 0:4 * NH17].rearrange("p (g m k) -> p g m k", g=4, k=17)
            width, lo = 17, 15
            sl = lambda X, g0, a, b_: X[:, g0:4, :, a:b_]
            sg1 = lambda X, g, a, b_: X[:, g, :, a:b_]
            u_b = uh
        else:
            for g in range(4):
                b = linear(Wu, 8, g, lambda c: xb1[:, c, 0:TW], TW, [R(xb1)])
                tt("dve", ub[:, g, 15:271], psb(b)[:, 0:TW], r1[:, 0:TW], ALU.mult, [R(PS, b), R(r1)], [R(ub, g)])
            cp("pool", ub[:, :, 0:15], uh[:].rearrange("p g (m k) -> p g m k", k=17)[:, :, r, 2:17],
               [R(uh)], [R(ub, "h")])
            n = TW
            uv, sa, sb2 = ub[:], s1[:], s2[:]
            width, lo = 271, 15
            sl = lambda X, g0, a, b_: X[:, g0:4, a:b_]
            sg1 = lambda X, g, a, b_: X[:, g, a:b_]
            u_b = ub
        cur, cur_b = uv, u_b
        other = [(sa, s1), (sb2, s2)]
        oi = 0
        for lvl, step in enumerate((1, 2, 4, 8)):
            dst, dst_b = other[oi]
            oi ^= 1
            tt("dve", sl(dst, lvl, step, width), sl(cur, lvl, step, width), sl(cur, lvl, 0, width - step),
               ALU.add, [R(cur_b)], [R(dst_b)])
            wv = float(2 ** (lvl + 1))
            ssrc = sg1(dst, lvl, lo, width)
            usrc = sg1(uv, lvl, lo, width)
            if halo:
                pdst = pooled[:, lvl, 0:n].rearrange("p (m k) -> p m k", k=2)
                if lvl == 3:
                    tt("dve", ssrc, ssrc, c16[:].rearrange("p (m k) -> p m k", k=2), ALU.mult,
                       [R(dst_b), R(c16)], [R(dst_b)])
            else:
                pdst = pooled[:, lvl, :]
            stt("dve", pdst, usrc, -wv, ssrc, ALU.mult, ALU.add, [R(u_b), R(dst_b)], [R(pooled, lvl)])
            cur, cur_b = dst, dst_b
        if stop == "D1a" or (stop == "M0a" and not halo):
            raise _Stop()
        shp = lambda ap: ap
        if halo:
            v17 = lambda ap: ap.rearrange("p c (m k) -> p c m k", k=17)[:, :, :, 15:17]
            v2 = lambda ap: ap.rearrange("p c (m k) -> p c m k", k=2)
            cp("dve", v2(xbq[:]), v17(xb1[:, :, 0:NH17]), [R(xb1)], [R(xbq)])
            cp("dve", v2(xtq[:]), v17(xt1[:, :, 0:NH17]), [R(xt1)], [R(xtq)])
            cp("dve", r1q[:].rearrange("p (m k) -> p m k", k=2),
               r1[:, 0:NH17].rearrange("p (m k) -> p m k", k=17)[:, :, 15:17], [R(r1)], [R(r1q)])
            xcol = lambda c: xbq[:, c, :]
            xcol_b = xbq
            r1v, r1v_b = r1q[:, :], r1q
            xres, xres_b = (lambda o: xtq[:, o, :]), xtq
            qcols = slice(NMAIN, NMAIN + NHQ)
            h1o, h1o_b = (lambda o: h1h[:, o, :]), h1h
        else:
            xcol = lambda c: xb1[:, c, 0:TW]
            xcol_b = xb1
            r1v, r1v_b = r1[:, 0:TW], r1
            xres, xres_b = (lambda o: xt1[:, o, 0:TW]), xt1
            qcols = slice(TW * r, TW * r + TW)
            h1o, h1o_b = (lambda o: xt1[:, o, 0:TW]), xt1
        for g in range(4):
            b = next_bank()
            mm(psb(b)[:, 0:n], Wgrp[:, g, :], pooled[:, g, 0:n], True, True, [R(pooled, g), R(Wgrp)], b)
            act(mixed[:, g, 0:n], psb(b)[:, 0:n], AF.Copy, [R(PS, b), R(pscw)], [R(mixed, g)], scale=pscw[:, g:g + 1])
        if stop == "D1b" or (stop == "M0b" and not halo):
            raise _Stop()
        for o in range(8):
            q = o % 2
            b = linear(Wga, 8, o, xcol, n, [R(xcol_b)])
            tt("dve", shp(tg[q][:, 0:n]), shp(psb(b)[:, 0:n]), r1v, ALU.mult, [R(PS, b), R(r1v_b)], [R(tg[q])])
            act(tg[q][:, 0:n], tg[q][:, 0:n], AF.Sigmoid, [R(tg[q])], [R(tg[q])])
            b = linear(Wab, 8, o, lambda c: attnT[:, c, qcols], n, [R(attnT)])
            tt("dve", m1[q][:, 0:n], psb(b)[:, 0:n], tg[q][:, 0:n], ALU.mult, [R(PS, b), R(tg[q])], [R(m1[q])])
            b = linear(Wgp, 8, o, xcol, n, [R(xcol_b)])
            tt("dve", shp(tg[q][:, 0:n]), shp(psb(b)[:, 0:n]), r1v, ALU.mult, [R(PS, b), R(r1v_b)], [R(tg[q])])
            act(tg[q][:, 0:n], tg[q][:, 0:n], AF.Sigmoid, [R(tg[q])], [R(tg[q])])
            b = linear(Wpb, 4, o, lambda c: mixed[:, c, 0:n], n, [R(mixed)])
            tt("dve", m2[q][:, 0:n], psb(b)[:, 0:n], tg[q][:, 0:n], ALU.mult, [R(PS, b), R(tg[q])], [R(m2[q])])
            tt("pool", merged[:, o, 0:n], m1[q][:, 0:n], m2[q][:, 0:n], ALU.add, [R(m1[q]), R(m2[q])], [R(merged, o)])
        if stop == "D1c" or (stop == "M0c" and not halo):
            raise _Stop()
        for o in range(8):
            b = linear(Wo, 8, o, lambda c: merged[:, c, 0:n], n, [R(merged)])
            tt("dve", shp(h1o(o)), shp(psb(b)[:, 0:n]), xres(o), ALU.add, [R(PS, b), R(xres_b)], [R(h1o_b, ("o", o))])
        src = h1h[:] if halo else xt1[:, :, 0:TW]
        dma(h1_scr, h1_scr[:].rearrange("(c p) t -> p c t", p=128)[:, :, qcols], h1o_b, src, no_waw=True)

    try:
        d1_tile("halo", 0)
        if stop == "D1d":
            raise _Stop()
        for r in range(ROWS):
            d1_tile("main", r)
            if stop == "M0d":
                raise _Stop()
    except _Stop:
        return finish()

    if stop == "D1":
        return finish()
    P.barrier()
    A.off = CONST_END
    Wup = wload("Wup", w_up, 0, 5632, 8)
    Wdn = A.alloc("Wdn", [128, 22, 1024], BF16)
    dma(Wdn, Wdn[:], w_down, w_down[:].rearrange("(c p) n -> p c n", p=128), eng="pool")
    uph = A.alloc("uph", [128, NCH_FF, NHQ], F32)
    h1b = A.alloc("h1b", [128, 8, TW], F32)
    hb = A.alloc("hb", [128, 8, TW], BF16)
    r2 = A.alloc("r2", [128, TW], F32)
    ln2 = A.alloc("ln2", [128, TW], F32)
    upv = [A.alloc("upv", [128, 258], F32) for _ in range(2)]
    upg = [A.alloc("upg", [128, 258], F32) for _ in range(2)]
    yv = [A.alloc("yv", [128, TW], F32) for _ in range(2)]
    yg = [A.alloc("yg", [128, TW], F32) for _ in range(2)]
    actT = A.alloc("actT", [128, 22, TW], BF16)
    h2 = A.alloc("h2", [128, 8, TW], F32)

    def d2_tile(kind, r):
        halo = kind == "halo"
        P.barrier()
        if halo:
            n, qcols = NHQ, slice(NMAIN, NMAIN + NHQ)
        else:
            n, qcols = TW, slice(TW * r, TW * r + TW)
        dma(h1b, h1b[:, :, 0:n], h1_scr, h1_scr[:].rearrange("(c p) t -> p c t", p=128)[:, :, qcols])
        act(hb[:, :, 0:n], h1b[:, :, 0:n], AF.Square, [R(h1b)], [R(hb)])
        norm_stats(hb, lambda c: hb[:, c, 0:n], n, r2, ln2)
        tt("dve", h2[:, :, 0:n], h1b[:, :, 0:n], r2[:, 0:n].unsqueeze(1).to_broadcast([128, 8, n]), ALU.mult,
           [R(h1b), R(r2)], [R(h2)])
        tt("pool", hb[:, :, 0:n], h2[:, :, 0:n], vec[:, G_FFN:G_FFN + 8].unsqueeze(2).to_broadcast([128, 8, n]),
           ALU.mult, [R(h2), R(vec)], [R(hb)])
        if halo:
            for c in range(NCH_FF):
                b = linear(Wup, 8, c, lambda k: hb[:, k, 0:n], n, [R(hb)])
                act(uph[:, c, :], psb(b)[:, 0:n], AF.Copy, [R(PS, b)], [R(uph, c)])
            return
        for c in range(22):
            q = c % 2
            for (cc, ubuf, ybuf) in ((c, upv[q], yv[q]), (22 + c, upg[q], yg[q])):
                b = linear(Wup, 8, cc, lambda k: hb[:, k, 0:TW], TW, [R(hb)])
                w0 = vec[:, CW + 3 * cc + 0:CW + 3 * cc + 1]
                w1 = vec[:, CW + 3 * cc + 1:CW + 3 * cc + 2]
                w2 = vec[:, CW + 3 * cc + 2:CW + 3 * cc + 3]
                bb = vec[:, CB + cc:CB + cc + 1]
                act(ubuf[:, 2:258], psb(b)[:, 0:TW], AF.Copy, [R(PS, b)], [R(ubuf, "m")])
                act(ybuf[:], psb(b)[:, 0:TW], AF.Identity, [R(PS, b), R(vec)], [R(ybuf)], scale=w2, bias=bb)
                cp("pool", ubuf[:, 0:2], uph[:, cc, 2 * r:2 * r + 2], [R(uph)], [R(ubuf, "h")])
                ur = [R(ubuf, "m"), R(ubuf, "h"), R(vec)]
                stt("dve", ybuf[:], ubuf[:, 1:257], w1, ybuf[:], ALU.mult, ALU.add, ur + [R(ybuf)], [R(ybuf)])
                stt("dve", ybuf[:], ubuf[:, 0:256], w0, ybuf[:], ALU.mult, ALU.add, ur + [R(ybuf)], [R(ybuf)])
            act(yg[q][:], yg[q][:], AF.Silu, [R(yg[q])], [R(yg[q])])
            tt("dve", actT[:, c, :], yg[q][:], yv[q][:], ALU.mult, [R(yg[q]), R(yv[q])], [R(actT, c)])
        for o in range(8):
            b = linear(Wdn, 22, o, lambda k: actT[:, k, :], TW, [R(actT)])
            tt("dve", h2[:, o, :], psb(b)[:, 0:TW], h1b[:, o, 0:TW], ALU.add, [R(PS, b), R(h1b)], [R(h2, o)])
        act(hb[:], h2[:], AF.Square, [R(h2)], [R(hb)])
        norm_stats(hb, lambda c: hb[:, c, :], TW, r2, ln2)
        tt("dve", h2[:], h2[:], r2[:, 0:TW].unsqueeze(1).to_broadcast([128, 8, TW]), ALU.mult,
           [R(h2), R(r2)], [R(h2)])
        tt("pool", h2[:], h2[:], vec[:, G_FIN:G_FIN + 8].unsqueeze(2).to_broadcast([128, 8, TW]), ALU.mult,
           [R(h2), R(vec)], [R(h2)])
        dma(yT, yT[:].rearrange("(c p) t -> p c t", p=128)[:, :, qcols], h2, h2[:], no_waw=True)

    d2_tile("halo", 0)
    for r in range(ROWS):
        d2_tile("main", r)

    return finish()


_CACHE = {}


def _rope_tables(npos):
    pos = np.arange(npos, dtype=np.float32)
    inv = (1.0 / (np.float32(10000.0) ** (np.arange(0, 64, 2, dtype=np.float32) / np.float32(64)))).astype(np.float32)
    ang = (pos[:, None] * inv[None, :]).astype(np.float32)
    return np.cos(ang).astype(np.float32), np.sin(ang).astype(np.float32)


def _core_layout(j):
    main_p = np.zeros(NMAIN, np.int64)
    h17_p = np.zeros(NH17, np.int64)
    for m in range(ROWS):
        qb = 4 * m + j
        main_p[m * BQ:(m + 1) * BQ] = 16 + BQ * qb + np.arange(BQ)
        h17_p[m * 17:(m + 1) * 17] = 16 + BQ * qb - 17 + np.arange(17)
    hq_p = h17_p.reshape(ROWS, 17)[:, 15:17].reshape(-1)
    return main_p, h17_p, hq_p


def _prep_shared(inputs):
    f = lambda a: np.ascontiguousarray(np.asarray(a, dtype=np.float32))
    cosf, sinf = _rope_tables(NPOS)
    vecs = np.zeros((128, 205), np.float32)
    vecs[:, 0:8] = f(inputs["g_mix"])[0].reshape(8, 128).T
    vecs[:, 8:16] = f(inputs["g_ffn"])[0].reshape(8, 128).T
    vecs[:, 16:24] = f(inputs["g_final"]).reshape(8, 128).T
    vecs[:, 24:28] = f(inputs["pool_scale"])[0].reshape(4, 128).T
    vecs[:, 28] = f(inputs["g_subln"])[0]
    cw = f(inputs["conv_w"])[0]
    vecs[:, 29:29 + 132] = cw.reshape(3, NCH_FF, 128).transpose(2, 1, 0).reshape(128, 132)
    vecs[:, 161:205] = f(inputs["conv_b"])[0].reshape(NCH_FF, 128).T
    shared = {
        "cosK": cosf, "sinK": sinf,
        "w_in": f(inputs["w_in"])[0], "lam": f(inputs["lam"])[0].reshape(1, 256), "vecs": vecs,
        "w_grp": f(inputs["w_pool_grp"])[0], "w_attn_br": f(inputs["w_attn_br"])[0],
        "w_pool_br": f(inputs["w_pool_br"])[0], "w_out": f(inputs["w_out"])[0],
        "w_up": f(inputs["w_up"])[0], "w_down": f(inputs["w_down"])[0],
    }
    return shared, cosf, sinf


def _prep_core(c, x, metaT, xT_b, shared, cosf, sinf):
    j = c % 4
    main_p, h17_p, hq_p = _core_layout(j)
    xT_all = xT_b
    own_p = np.concatenate([main_p, h17_p, hq_p])
    valid = own_p >= 0
    xT_own = np.zeros((D, NOWN), np.float32)
    xT_own[:, valid] = xT_all[:, own_p[valid]]
    qpos = np.concatenate([main_p, hq_p])
    k = np.arange(128)[:, None, None]
    i = np.arange(8)[None, :, None]
    q = np.arange(BQ)[None, None, :]
    maskM = ((128 * i + k) <= (BQ * j + q)).astype(np.float32).reshape(128, 8 * BQ)
    keyp = np.full((65, 128), 1 << 30, np.int64)
    keyp[0, :16] = np.arange(16)
    keyp[1:, :] = 16 + 128 * np.arange(64)[:, None] + np.arange(128)[None, :]
    maskH = (keyp.T[:, :, None] <= hq_p[None, None, :]).astype(np.float32).reshape(128, 65 * NHQ)
    cnt16 = np.ones((128, NHQ), np.float32)
    cnt16[:, :] = (16.0 / np.minimum(16, hq_p + 1).astype(np.float64)).astype(np.float32)[None, :]
    m = dict(shared)
    m.update({
        "xT_all": xT_all, "xT_own": xT_own,
        "cosQ": np.ascontiguousarray(cosf[qpos]), "sinQ": np.ascontiguousarray(sinf[qpos]),
        "maskM": maskM, "maskH": maskH, "cnt16": cnt16,
    })
    return m


def kernel(**inputs):
    dbg = bool(inputs.pop("_dbg", False))
    stop = inputs.pop("_stop", None)
    cores = inputs.pop("_cores", list(range(8)))
    x = np.asarray(inputs["x"], dtype=np.float32)
    meta = np.asarray(inputs["meta_tokens"], dtype=np.float32)
    key = ("prog", dbg, stop)
    if key not in _CACHE:
        _CACHE[key] = build_program(dbg, stop)[0]
    nc = _CACHE[key]
    shared, cosf, sinf = _prep_shared(inputs)
    xT = [np.ascontiguousarray(np.concatenate([meta, x[b]], axis=0).T) for b in range(2)]
    in_maps = [_prep_core(c, x, None, xT[c // 4], shared, cosf, sinf) for c in cores]
    res = run_bass_kernel_spmd(nc, in_maps, core_ids=list(range(len(cores))))
    out = np.zeros((2, SEQ, D), np.float32)
    for ci, c in enumerate(cores):
        b, j = c // 4, c % 4
        yT = np.asarray(res.results[ci]["yT"])
        for m in range(ROWS):
            qb = 4 * m + j
            out[b, BQ * qb:BQ * (qb + 1), :] = yT[:, m * BQ:(m + 1) * BQ].T
    if dbg:
        return out, res
    return out
```

```python
import contextlib
import numpy as np
import concourse.bass as bass
import concourse.mybir as mybir
from concourse.bass_utils import run_bass_kernel_spmd

F32 = mybir.dt.float32
BF16 = mybir.dt.bfloat16
AF = mybir.ActivationFunctionType
ALU = mybir.AluOpType

ENGS = ("pe", "act", "dve", "pool", "sp")

D = 1024
SEQ = 8192
NMETA = 16
NPOS = NMETA + SEQ
NH = 8
DFF = 2816
NCH_FF = 44
EPS = 1e-6
ROWS = 8
BQ = 256
NMAIN = ROWS * BQ
NH17 = ROWS * 17
NHQ = ROWS * 2
NOWN = NMAIN + NH17 + NHQ
NQ = NMAIN + NHQ
LAM_INIT = 0.2


class Buf:
    def __init__(self, t, name, exclusive=False):
        self.t = t
        self.name = name
        self.regions = {}
        self.exclusive = exclusive

    def __getitem__(self, idx):
        return self.t[idx]


class _Region:
    __slots__ = ("last_write", "reads", "sem", "ndma")

    def __init__(self):
        self.last_write = None
        self.reads = []
        self.sem = None
        self.ndma = 0


class _Op:
    __slots__ = ("eng", "fn", "deps", "is_dma", "signal", "region", "dma_idx", "sig_idx", "idx")


def R(buf, key=None):
    return (buf, key)


class Prog:
    def __init__(self, nc):
        self.nc = nc
        self.ops = []
        self.stack = contextlib.ExitStack()
        self.last_on_eng = {}
        self.barrier_deps = {}

    def dram(self, name, shape, dtype, kind="Internal"):
        t = self.nc.dram_tensor(name, list(shape), dtype, kind=kind)
        return Buf(t.ap(), name)

    def _regs(self, acc):
        buf, key = acc
        if key is None:
            out = [buf.regions.setdefault(None, _Region())]
            out += [r for k, r in buf.regions.items() if k is not None]
            return out
        out = [buf.regions.setdefault(key, _Region())]
        if None in buf.regions:
            out.append(buf.regions[None])
        return out

    def barrier(self):
        deps = set(self.last_on_eng.values())
        seen = set()
        for o in self.ops:
            if o.is_dma:
                seen.add(id(o.region))
        last_dma = {}
        for o in self.ops:
            if o.is_dma:
                last_dma[id(o.region)] = o.idx
        deps.update(last_dma.values())
        for e in ENGS:
            self.barrier_deps[e] = set(deps) | self.barrier_deps.get(e, set())

    def op(self, eng, fn, reads=(), writes=(), dma=False, no_waw=False):
        o = _Op()
        o.eng, o.fn, o.is_dma, o.signal = eng, fn, dma, False
        o.idx = len(self.ops)
        reads = list(reads)
        writes = list(writes)
        excl = [a for a in reads if a[0].exclusive]
        if excl:
            reads = [a for a in reads if not a[0].exclusive]
            writes = writes + [a for a in excl if a not in writes]
        deps = set()
        for acc in reads:
            for r in self._regs(acc):
                if r.last_write is not None:
                    deps.add(r.last_write)
        for acc in writes:
            for r in self._regs(acc):
                if r.last_write is not None and not no_waw:
                    deps.add(r.last_write)
                deps.update(r.reads)
        if eng in self.barrier_deps:
            deps.update(self.barrier_deps.pop(eng))
        deps.discard(o.idx)
        o.deps = deps
        o.region = None
        if dma:
            buf, key = writes[0]
            o.region = buf.regions.setdefault(key, _Region())
        for acc in reads:
            buf, key = acc
            rl = buf.regions.setdefault(key, _Region()).reads
            if not dma:
                for i_, prev in enumerate(rl):
                    po = self.ops[prev]
                    if (not po.is_dma) and po.eng == eng:
                        rl[i_] = o.idx
                        break
                else:
                    rl.append(o.idx)
            else:
                rl.append(o.idx)
        for acc in writes:
            buf, key = acc
            reg = buf.regions.setdefault(key, _Region())
            reg.reads = []
            reg.last_write = o.idx
            if key is None:
                for k, r in buf.regions.items():
                    if k is not None:
                        r.last_write = o.idx
                        r.reads = []
        self.ops.append(o)
        if not dma:
            self.last_on_eng[eng] = o.idx
        return o

    def emit(self):
        nc = self.nc
        ops = self.ops
        for o in ops:
            for d in o.deps:
                od = ops[d]
                if od.eng == "pe" and o.eng == "pe" and not od.is_dma:
                    continue
                od.signal = True
        cnt = {e: 0 for e in ENGS}
        for o in ops:
            if o.is_dma:
                o.region.ndma += 1
                o.dma_idx = o.region.ndma
            elif o.signal:
                cnt[o.eng] += 1
                o.sig_idx = cnt[o.eng]
        self.sig_counts = cnt
        sems = {e: self.stack.enter_context(nc.semaphore(f"s_{e}")) for e in ENGS}
        nreg = 0
        for o in ops:
            if o.is_dma and o.region.sem is None:
                nreg += 1
                o.region.sem = self.stack.enter_context(nc.semaphore(f"d_{nreg}"))
        self.n_dma_sems = nreg
        per_eng = {e: [o for o in ops if o.eng == e] for e in ENGS}

        def run(ename, eng):
            waited = {}
            for o in per_eng[ename]:
                for d in sorted(o.deps):
                    od = ops[d]
                    if od.is_dma:
                        key = ("d", id(od.region))
                        val = 16 * od.dma_idx
                        sem = od.region.sem
                    else:
                        if od.eng == "pe" and ename == "pe":
                            continue
                        key = ("e", od.eng)
                        val = od.sig_idx
                        sem = sems[od.eng]
                    if waited.get(key, 0) >= val:
                        continue
                    waited[key] = val
                    eng.wait_ge(sem, val)
                ins = o.fn(eng)
                if o.is_dma:
                    ins.then_inc(o.region.sem, 16)
                elif o.signal:
                    ins.then_inc(sems[ename], 1)
            return waited

        with nc.Block() as block:
            @block.tensor
            def _(e):
                run("pe", e)

            @block.scalar
            def _(e):
                run("act", e)

            @block.vector
            def _(e):
                run("dve", e)

            @block.gpsimd
            def _(e):
                run("pool", e)

            @block.sync
            def _(e):
                w = run("sp", e)
                seen = set()
                for o in ops:
                    if o.is_dma and id(o.region) not in seen:
                        seen.add(id(o.region))
                        val = 16 * o.region.ndma
                        if w.get(("d", id(o.region)), 0) < val:
                            e.wait_ge(o.region.sem, val)

    def close(self):
        self.stack.close()


def build_program(dbg=False, stop=None):
    nc = bass.Bass("TRN2", target_bir_lowering=False)
    P = Prog(nc)
    IN = "ExternalInput"

    class _Stop(Exception):
        pass

    def finish():
        P.emit()
        P.close()
        return nc, P

    xT_all = P.dram("xT_all", [D, NMETA + SEQ], F32, IN)
    xT_own = P.dram("xT_own", [D, NOWN], F32, IN)
    cosK = P.dram("cosK", [NPOS, 32], F32, IN)
    sinK = P.dram("sinK", [NPOS, 32], F32, IN)
    cosQ = P.dram("cosQ", [NQ, 32], F32, IN)
    sinQ = P.dram("sinQ", [NQ, 32], F32, IN)
    maskM = P.dram("maskM", [128, 8 * BQ], F32, IN)
    maskH = P.dram("maskH", [128, 65 * NHQ], F32, IN)
    cnt16 = P.dram("cnt16", [128, NHQ], F32, IN)
    w_in = P.dram("w_in", [D, 5632], F32, IN)
    lam = P.dram("lam", [1, 256], F32, IN)
    vecs = P.dram("vecs", [128, 8 + 8 + 8 + 4 + 1 + 44 * 3 + 44], F32, IN)
    w_grp = P.dram("w_grp", [4, 128, 128], F32, IN)
    w_attn_br = P.dram("w_attn_br", [D, D], F32, IN)
    w_pool_br = P.dram("w_pool_br", [512, D], F32, IN)
    w_out = P.dram("w_out", [D, D], F32, IN)
    w_up = P.dram("w_up", [D, 5632], F32, IN)
    w_down = P.dram("w_down", [DFF, D], F32, IN)
    yT = P.dram("yT", [D, NMAIN], F32, "ExternalOutput")
    kT_scr = P.dram("kT_scr", [NH, 128, NPOS], BF16)
    v_scr = P.dram("v_scr", [NH, 128, 64, 128], BF16)
    vm_scr = P.dram("vm_scr", [NH, 16, 128], BF16)
    h1_scr = P.dram("h1_scr", [D, NQ], F32, "ExternalOutput" if dbg else "Internal")
    dbgT = {}
    if dbg:
        dbgT["QT"] = P.dram("dbg_QT", [128, NH * NQ], BF16, "ExternalOutput")
        dbgT["attnT"] = P.dram("dbg_attnT", [128, NH * NQ], BF16, "ExternalOutput")
        dbgT["kT"] = P.dram("dbg_kT", [128, NPOS], BF16, "ExternalOutput")
        dbgT["v"] = P.dram("dbg_v", [128, 64 * 128], BF16, "ExternalOutput")

    ARENA_F32 = 51200
    arena_t = P.stack.enter_context(nc.sbuf_tensor("arena", [128, ARENA_F32], F32))
    psum_t = P.stack.enter_context(nc.psum_tensor("psum", [128, 4096], F32))
    PS = Buf(psum_t, "psum", exclusive=True)

    class Arena:
        def __init__(self):
            self.off = 0
            self.n = 0

        def alloc(self, name, shape, dtype):
            esz = 4 if dtype == F32 else 2
            nfree = int(np.prod(shape[1:]))
            nbytes = (nfree * esz + 31) // 32 * 32
            o4 = self.off // 4
            n4 = nbytes // 4
            assert o4 + n4 <= ARENA_F32, f"arena overflow at {name}: {self.off + nbytes}"
            ap = arena_t[:, o4:o4 + n4]
            if dtype != F32:
                ap = ap.bitcast(dtype)
            ap = ap[:, 0:nfree]
            if len(shape) > 2:
                names = " ".join(f"d{i}" for i in range(len(shape) - 1))
                kw = {f"d{i}": shape[i + 1] for i in range(len(shape) - 1)}
                ap = ap.rearrange(f"p ({names}) -> p {names}", **kw)
            self.off += nbytes
            self.n += 1
            return Buf(ap, f"{name}_{self.n}")

    A = Arena()

    def psb(bank, ncol=512, col0=0, dtype=F32):
        ap = psum_t[:, bank * 512 + col0: bank * 512 + col0 + ncol]
        return ap

    bank_rr = [0]

    def next_bank(lo=0, hi=8):
        b = lo + bank_rr[0] % (hi - lo)
        bank_rr[0] += 1
        return b

    def dma(out_b, out_ap, in_b, in_ap, eng="sp", okey=None, ikey=None, no_waw=False):
        P.op(eng, lambda e: e.dma_start(out=out_ap, in_=in_ap), reads=[R(in_b, ikey)],
             writes=[R(out_b, okey)], dma=True, no_waw=no_waw)

    def mm(out_ap, lhsT, rhs, start, stop, reads, bank):
        P.op("pe", lambda e: e.matmul(out_ap, lhsT=lhsT, rhs=rhs, start=start, stop=stop),
             reads=reads, writes=[R(PS, bank)])

    def act(out_ap, in_ap, func, reads, writes, scale=1.0, bias=None):
        if bias is None:
            P.op("act", lambda e: e.activation(out=out_ap, in_=in_ap, func=func, scale=scale),
                 reads=reads, writes=writes)
        else:
            P.op("act", lambda e: e.activation(out=out_ap, in_=in_ap, func=func, scale=scale, bias=bias),
                 reads=reads, writes=writes)

    def tt(eng, out_ap, in0, in1, op, reads, writes):
        P.op(eng, lambda e: e.tensor_tensor(out=out_ap, in0=in0, in1=in1, op=op), reads=reads, writes=writes)

    def ts(eng, out_ap, in0, s1, s2, op0, op1, reads, writes):
        if op1 is None:
            P.op(eng, lambda e: e.tensor_scalar(out=out_ap, in0=in0, scalar1=s1, scalar2=None, op0=op0),
                 reads=reads, writes=writes)
        else:
            P.op(eng, lambda e: e.tensor_scalar(out=out_ap, in0=in0, scalar1=s1, scalar2=s2, op0=op0, op1=op1),
                 reads=reads, writes=writes)

    def stt(eng, out_ap, in0, scalar, in1, op0, op1, reads, writes):
        P.op(eng, lambda e: e.scalar_tensor_tensor(out=out_ap, in0=in0, scalar=scalar, in1=in1, op0=op0, op1=op1),
             reads=reads, writes=writes)

    def tr(out_ap, in_ap, id_ap, reads, bank):
        P.op("pe", lambda e: e.transpose(out_ap, in_ap, id_ap), reads=reads, writes=[R(PS, bank)])

    def cp(eng, out_ap, in_ap, reads, writes):
        P.op(eng, lambda e: e.tensor_copy(out=out_ap, in_=in_ap), reads=reads, writes=writes)

    def rsqrt_act(out_b, out_ap, in_ap, in_reads, tmp_b, tmp_ap, inv_n):
        act(tmp_ap, in_ap, AF.Ln, in_reads, [R(tmp_b)], scale=inv_n, bias=EPS)
        act(out_ap, tmp_ap, AF.Exp, [R(tmp_b)], [R(out_b)], scale=-0.5)

    ident = A.alloc("ident", [128, 128], BF16)
    ones = A.alloc("ones", [128, 128], BF16)
    idf = A.alloc("idf", [128, 128], F32)
    vec = A.alloc("vec", [128, 205], F32)
    lamb = A.alloc("lamb", [128, 256], F32)
    lprod = A.alloc("lprod", [128, 128], F32)
    lsum = A.alloc("lsum", [128, 2], F32)
    neglam = A.alloc("neglam", [128, 1], F32)
    gsub08 = A.alloc("gsub08", [128, 1], F32)
    pscw = A.alloc("pscw", [128, 4], F32)
    mM = A.alloc("mM", [128, 8, BQ], BF16)
    mH = A.alloc("mH", [128, 65, NHQ], BF16)
    c16 = A.alloc("c16", [128, NHQ], F32)
    G_MIX, G_FFN, G_FIN, PSC, GSUB, CW, CB = 0, 8, 16, 24, 28, 29, 29 + 132

    P.op("pool", lambda e: e.memset(idf[:], 1.0), writes=[R(idf)])
    P.op("pool", lambda e: e.affine_select(out=idf[:], in_=idf[:], pattern=[[-1, 128]], compare_op=ALU.is_equal,
                                           fill=0.0, base=0, channel_multiplier=1), reads=[R(idf)], writes=[R(idf)])
    cp("dve", ident[:], idf[:], [R(idf)], [R(ident)])
    P.op("dve", lambda e: e.memset(ones[:], 1.0), writes=[R(ones)])
    dma(vec, vec[:], vecs, vecs[:])
    dma(lamb, lamb[:], lam, lam[:].to_broadcast([128, 256]))
    dma(c16, c16[:], cnt16, cnt16[:])
    dma(mM, mM[:], maskM, maskM[:].rearrange("p (i q) -> p i q", i=8), eng="pool")
    dma(mH, mH[:], maskH, maskH[:].rearrange("p (b q) -> p b q", b=65), eng="pool")
    lv = lamb[:].rearrange("p (a b d) -> p a b d", a=2, b=2)
    tt("dve", lprod[:].rearrange("p (a d) -> p a d", a=2), lv[:, :, 0, :], lv[:, :, 1, :], ALU.mult,
       [R(lamb)], [R(lprod)])
    P.op("dve", lambda e: e.reduce_sum(out=lsum[:], in_=lprod[:].rearrange("p (a d) -> p a d", a=2),
                                       axis=mybir.AxisListType.X), reads=[R(lprod)], writes=[R(lsum)])
    act(lsum[:], lsum[:], AF.Exp, [R(lsum)], [R(lsum)])
    stt("dve", neglam[:], lsum[:, 1:2], -LAM_INIT, lsum[:, 0:1], ALU.add, ALU.subtract, [R(lsum)], [R(neglam)])
    ts("dve", gsub08[:], vec[:, GSUB:GSUB + 1], 1.0 - LAM_INIT, None, ALU.mult, None, [R(vec)], [R(gsub08)])
    for g in range(4):
        ts("dve", pscw[:, g:g + 1], vec[:, PSC + g:PSC + g + 1], 1.0 / (2 ** (g + 1)), None, ALU.mult, None,
           [R(vec)], [R(pscw, g)])
    CONST_END = A.off

    QT = A.alloc("QT", [128, NH, NQ], BF16)
    QT_END = A.off
    attnT = A.alloc("attnT", [128, NH, NQ], BF16)
    ATT_END = A.off

    def w_view(wb, lo, hi):
        return wb[:].rearrange("(c p) n -> p c n", p=128)[:, :, lo:hi]

    def proj_phase(tiles, W_list, sink):
        xt = [A.alloc("xt", [128, 8, 512], F32) for _ in range(2)]
        xb = [A.alloc("xb", [128, 8, 512], BF16) for _ in range(2)]
        xsq1 = A.alloc("xsq", [128, 8, 512], BF16)
        xsq = [xsq1, xsq1]
        ct = [A.alloc("ct", [128, 4, 32], F32) for _ in range(2)]
        st = [A.alloc("st", [128, 4, 32], F32) for _ in range(2)]
        r4 = [A.alloc("r4", [128, 4], F32) for _ in range(2)]
        lnr = [A.alloc("lnr", [128, 4], F32) for _ in range(2)]
        kf = [A.alloc("kf", [128, 1024], F32) for _ in range(2)]
        tA1 = A.alloc("tA", [128, 1024], F32)
        tB1 = A.alloc("tB", [128, 1024], F32)
        tA = [tA1, tA1]
        tB = [tB1, tB1]
        kb16 = [A.alloc("kb16", [128, 1024], BF16) for _ in range(2)]
        gmix_bc = vec[:, G_MIX:G_MIX + 8].unsqueeze(2)
        bigs = [(0, 1), (2, 3), (4, 5)]
        big_i = [0]
        RB, TB = 6, 7
        bcount = [0]

        def load(ti):
            src, col0, ncols, blocks, cosd, sind, row0 = tiles[ti]
            s = ti % 2
            dma(xt[s], xt[s][:, :, 0:ncols], src,
                src[:].rearrange("(c p) t -> p c t", p=128)[:, :, col0:col0 + ncols])
            for bi, (boff, nb) in enumerate(blocks):
                dma(ct[s], ct[s][0:nb, bi, :], cosd, cosd[row0 + boff:row0 + boff + nb, :], okey=bi)
                dma(st[s], st[s][0:nb, bi, :], sind, sind[row0 + boff:row0 + boff + nb, :], okey=bi)

        pending = []

        def flush_one():
            ti_, wi_, bi_, boff_, nb_, q_ = pending.pop(0)
            tps = psb(TB).bitcast(BF16)
            for h in range(NH):
                tr(tps[:, h * 128:h * 128 + nb_], kb16[q_][0:nb_, h * 128:(h + 1) * 128], ident[0:nb_, 0:nb_],
                   [R(kb16[q_], 0), R(kb16[q_], 1), R(ident)], TB)
            tview = tps[:, 0:1024].rearrange("p (h t) -> p h t", h=NH)[:, :, 0:nb_]
            sink(ti_, wi_, bi_, boff_, nb_, (tview, [R(PS, TB)], None, None))

        load(0)
        for ti in range(len(tiles)):
            src, col0, ncols, blocks, cosd, sind, row0 = tiles[ti]
            s = ti % 2
            if ti + 1 < len(tiles):
                load(ti + 1)
            tt("dve", xb[s][:, :, 0:ncols], xt[s][:, :, 0:ncols], gmix_bc.to_broadcast([128, 8, ncols]), ALU.mult,
               [R(xt[s]), R(vec)], [R(xb[s])])
            act(xsq[s][:, :, 0:ncols], xt[s][:, :, 0:ncols], AF.Square, [R(xt[s])], [R(xsq[s])])
            for bi, (boff, nb) in enumerate(blocks):
                for c in range(8):
                    mm(psb(RB)[0:nb, bi:bi + 1], xsq[s][:, c, boff:boff + nb], ones[:, 0:1], c == 0, c == 7,
                       [R(xsq[s]), R(ones)], RB)
            nbl = len(blocks)
            nb0 = blocks[0][1]
            act(lnr[s][0:nb0, 0:nbl], psb(RB)[0:nb0, 0:nbl], AF.Ln, [R(PS, RB)], [R(lnr[s])], scale=1.0 / D, bias=EPS)
            act(r4[s][0:nb0, 0:nbl], lnr[s][0:nb0, 0:nbl], AF.Exp, [R(lnr[s])], [R(r4[s])], scale=-0.5)
            for bi, (boff, nb) in enumerate(blocks):
                for wi, (Wsb, kind) in enumerate(W_list):
                    bk = bigs[big_i[0] % 3]
                    big_i[0] += 1
                    for half in range(2):
                        for c in range(8):
                            mm(psb(bk[half])[0:nb, :], xb[s][:, c, boff:boff + nb],
                               Wsb[:, c, half * 512:(half + 1) * 512], c == 0, c == 7,
                               [R(xb[s]), R(Wsb)], bk[half])
                    big_ap = psum_t[0:nb, bk[0] * 512:bk[0] * 512 + 1024]
                    q = bcount[0] % 2
                    if kind == "rope":
                        bcount[0] += 1
                    if kind == "plain":
                        sink(ti, wi, bi, boff, nb, (big_ap, [R(PS, bk[0]), R(PS, bk[1])], r4[s][0:nb, bi:bi + 1], R(r4[s])))
                        continue
                    act(kf[q][0:nb, :], big_ap, AF.Copy, [R(PS, bk[0]), R(PS, bk[1]), R(r4[s])], [R(kf[q])],
                        scale=r4[s][0:nb, bi:bi + 1])
                    kv = kf[q][0:nb, :].rearrange("p (a b d) -> p a b d", a=16, b=2)
                    av = tA[q][0:nb, :].rearrange("p (a b d) -> p a b d", a=16, b=2)
                    bv = tB[q][0:nb, :].rearrange("p (a b d) -> p a b d", a=16, b=2)
                    ov = kb16[q][0:nb, :].rearrange("p (a b d) -> p a b d", a=16, b=2)
                    cosb = ct[s][0:nb, bi, :]
                    sinb = st[s][0:nb, bi, :]
                    tt("pool", av, kv, cosb.unsqueeze(1).unsqueeze(1).to_broadcast([nb, 16, 2, 32]), ALU.mult,
                       [R(kf[q]), R(ct[s], bi)], [R(tA[q])])
                    sb_ = sinb.unsqueeze(1).to_broadcast([nb, 16, 32])
                    tt("dve", bv[:, :, 0, :], kv[:, :, 1, :], sb_, ALU.mult, [R(kf[q]), R(st[s], bi)], [R(tB[q], 0)])
                    tt("dve", bv[:, :, 1, :], kv[:, :, 0, :], sb_, ALU.mult, [R(kf[q]), R(st[s], bi)], [R(tB[q], 1)])
                    tt("dve", ov[:, :, 0, :], av[:, :, 0, :], bv[:, :, 0, :], ALU.subtract,
                       [R(tA[q]), R(tB[q], 0)], [R(kb16[q], 0)])
                    tt("dve", ov[:, :, 1, :], av[:, :, 1, :], bv[:, :, 1, :], ALU.add,
                       [R(tA[q]), R(tB[q], 1)], [R(kb16[q], 1)])
                    pending.append((ti, wi, bi, boff, nb, q))
                while len(pending) > 1:
                    flush_one()
        while pending:
            flush_one()

    A.off = QT_END
    Wk = A.alloc("Wk", [128, 8, 1024], BF16)
    Wv = A.alloc("Wv", [128, 8, 1024], BF16)
    dma(Wk, Wk[:], w_in, w_view(w_in, 1024, 2048), eng="pool")
    dma(Wv, Wv[:], w_in, w_view(w_in, 2048, 3072), eng="pool")
    kst = [A.alloc("kst", [128, NH, 512], BF16) for _ in range(2)]
    vst = [A.alloc("vst", [128, 4, 1024], BF16) for _ in range(2)]
    tilesA = [(xT_all, 0, 16, [(0, 16)], cosK, sinK, 0)]
    for t in range(16):
        tilesA.append((xT_all, 16 + 512 * t, 512, [(128 * b, 128) for b in range(4)], cosK, sinK, 16 + 512 * t))

    def sinkA(ti, wi, bi, boff, nb, data):
        s = ti % 2
        src_ap, src_reads, rcol, rread = data
        if wi == 0:
            act(kst[s][:, :, boff:boff + nb], src_ap, AF.Copy, src_reads, [R(kst[s], bi)])
            last = (bi == len(tilesA[ti][3]) - 1)
            if last:
                ncols = tilesA[ti][2]
                col0 = tilesA[ti][1]
                dma(kT_scr, kT_scr[:].rearrange("h f t -> f h t")[:, :, col0:col0 + ncols],
                    kst[s], kst[s][:, :, 0:ncols], no_waw=True)
        else:
            act(vst[s][0:nb, bi, :], src_ap, AF.Copy, src_reads + [rread], [R(vst[s], bi)], scale=rcol)
            if ti == 0:
                dma(vm_scr, vm_scr[:].rearrange("h t e -> t h e"),
                    vst[s], vst[s][0:16, 0, :].rearrange("t (h e) -> t h e", h=NH), ikey=bi)
            else:
                kb = 4 * (ti - 1) + bi
                dma(v_scr, v_scr[:, :, kb, :].rearrange("h t e -> t h e"),
                    vst[s], vst[s][:, bi, :].rearrange("t (h e) -> t h e", h=NH), ikey=bi, no_waw=True)

    proj_phase(tilesA, [(Wk, "rope"), (Wv, "plain")], sinkA)
    if stop == "A":
        if dbg:
            KTd = A.alloc("KTd", [128, NPOS], BF16)
            dma(KTd, KTd[:], kT_scr, kT_scr[0])
            dma(dbgT["kT"], dbgT["kT"][:], KTd, KTd[:])
        return finish()

    P.barrier()
    A.off = QT_END
    Wq = A.alloc("Wq", [128, 8, 1024], BF16)
    dma(Wq, Wq[:], w_in, w_view(w_in, 0, 1024), eng="pool")
    tilesB = []
    for t in range(4):
        tilesB.append((xT_own, 512 * t, 512, [(128 * b, 128) for b in range(4)], cosQ, sinQ, 512 * t))
    tilesB.append((xT_own, NMAIN + NH17, NHQ, [(0, NHQ)], cosQ, sinQ, NMAIN))

    def sinkB(ti, wi, bi, boff, nb, data):
        src_ap, src_reads, _, _ = data
        q0 = (512 * ti if ti < 4 else NMAIN) + boff
        act(QT[:, :, q0:q0 + nb], src_ap, AF.Copy, src_reads, [R(QT, (ti, bi))])

    proj_phase(tilesB, [(Wq, "rope")], sinkB)
    if dbg:
        dma(dbgT["QT"], dbgT["QT"][:], QT, QT[:].rearrange("p h q -> p (h q)"))

    if stop == "B":
        return finish()
    P.barrier()
    A.off = ATT_END
    KT = [A.alloc("KT", [128, NPOS], BF16) for _ in range(2)]
    VV = [A.alloc("VV", [128, 64, 128], BF16) for _ in range(2)]
    VM = [A.alloc("VM", [128, 128], BF16) for _ in range(2)]
    Pt = [A.alloc("Pt", [128, 2, 512], BF16) for _ in range(3)]
    Ps = [A.alloc("Ps", [128, 2, 256], BF16) for _ in range(3)]
    Osb = A.alloc("Osb", [128, 512], F32)
    Lsb = A.alloc("Lsb", [128, 512], F32)
    t0b = A.alloc("t0b", [128, 256], F32)
    t1b = A.alloc("t1b", [128, 256], F32)
    ob = A.alloc("ob", [128, 256], F32)
    osq = A.alloc("osq", [128, 256], BF16)
    lnb = A.alloc("lnb", [128, 256], F32)
    rsb = A.alloc("rsb", [128, 256], F32)
    SBK = [(0, 1), (2, 3), (4, 5)]
    OB, LB, EB = 6, 7, 4
    NSB = 3

    def load_head(h):
        s = h % 2
        dma(KT[s], KT[s][:], kT_scr, kT_scr[h])
        dma(VV[s], VV[s][:], v_scr, v_scr[h])
        dma(VM[s], VM[s][0:16, :], vm_scr, vm_scr[h])

    def attention_tile(h, q0, nq, nkb, kps, mask_of):
        s = h % 2
        kt, vv, vm = KT[s], VV[s], VM[s]
        q_r = [R(QT)]
        steps = [(-1, 1)] + [(kb0, kps) for kb0 in range(0, nkb, kps)]
        nst = len(steps)
        w = kps * nq

        def qk(si):
            kb0, n = steps[si]
            sb = SBK[si % NSB]
            for kk in range(n):
                if kb0 < 0:
                    kcols, np_ = slice(0, 16), 16
                else:
                    c0 = 16 + 128 * (kb0 + kk)
                    kcols, np_ = slice(c0, c0 + 128), 128
                for comp in range(2):
                    mm(psb(sb[comp])[0:np_, kk * nq:(kk + 1) * nq], kt[comp * 64:(comp + 1) * 64, kcols],
                       QT[comp * 64:(comp + 1) * 64, h, q0:q0 + nq], True, True, q_r + [R(kt)], sb[comp])

        def expo(si):
            kb0, n = steps[si]
            sb = SBK[si % NSB]
            np_ = 16 if kb0 < 0 else 128
            pin = psum_t[0:np_, sb[0] * 512:sb[0] * 512 + 1024].rearrange("p (c w) -> p c w", c=2)[:, :, 0:n * nq]
            pt = Pt[si % NSB]
            act(pt[0:np_, :, 0:n * nq], pin, AF.Exp, [R(PS, sb[0]), R(PS, sb[1])], [R(pt)], scale=0.125)
            ptv = pt[0:np_, :, 0:n * nq].rearrange("p c (k q) -> p c k q", k=n)
            if kb0 < 0:
                m = mask_of(-1)
                if m is not None:
                    tt("dve", ptv[:, :, 0, :], ptv[:, :, 0, :], m[0].unsqueeze(1).to_broadcast([np_, 2, nq]), ALU.mult,
                       [R(pt), m[1]], [R(pt)])
            else:
                ms = [mask_of(kb0 + kk) for kk in range(n)]
                if all(m is not None for m in ms) and n > 1 and ms[0][2] is not None:
                    mb, mr, (mbuf, b0) = ms[0]
                    mall = mbuf[:, b0:b0 + n, :]
                    tt("dve", ptv, ptv, mall.unsqueeze(1).to_broadcast([np_, 2, n, nq]), ALU.mult,
                       [R(pt), mr], [R(pt)])
                else:
                    for kk, m in enumerate(ms):
                        if m is not None:
                            tt("dve", ptv[:, :, kk, :], ptv[:, :, kk, :],
                               m[0].unsqueeze(1).to_broadcast([np_, 2, nq]), ALU.mult, [R(pt), m[1]], [R(pt)])

        def pv(si):
            kb0, n = steps[si]
            pt = Pt[si % NSB]
            np_ = 16 if kb0 < 0 else 128
            ptv = pt[0:np_, :, 0:n * nq].rearrange("p c (k q) -> p c k q", k=n)
            pair = (n == 2 and kb0 >= 0)
            if pair:
                ps2 = Ps[si % NSB]
                tt("dve", ps2[:, :, 0:nq], ptv[:, :, 0, :], ptv[:, :, 1, :], ALU.add, [R(pt)], [R(ps2)])
            for kk in range(n):
                first = (si == 0 and kk == 0)
                last = (si == nst - 1 and kk == n - 1)
                lhs_v = vm[0:16, :] if kb0 < 0 else vv[:, kb0 + kk, :]
                mm(psb(OB)[:, 0:2 * nq], lhs_v, ptv[:, :, kk, :], first, last, [R(pt), R(vm if kb0 < 0 else vv)], OB)
                if pair:
                    if kk == 1:
                        mm(psb(LB)[:, 0:2 * nq], ones[:, :], ps2[:, :, 0:nq], False, last, [R(ps2), R(ones)], LB)
                else:
                    mm(psb(LB)[:, 0:2 * nq], ones[0:np_, :], ptv[:, :, kk, :], first, last, [R(pt), R(ones)], LB)

        for si in range(min(2, nst)):
            qk(si)
        for si in range(nst):
            if si + 2 < nst:
                qk(si + 2)
            expo(si)
            pv(si)
        n2 = 2 * nq
        cp("dve", Lsb[:, 0:n2], psb(LB)[:, 0:n2], [R(PS, LB)], [R(Lsb)])
        cp("dve", Osb[:, 0:n2], psb(OB)[:, 0:n2], [R(PS, OB)], [R(Osb)])
        P.op("dve", lambda e: e.reciprocal(out=Lsb[:, 0:n2], in_=Lsb[:, 0:n2]), reads=[R(Lsb)], writes=[R(Lsb)])
        tt("dve", t0b[:, 0:nq], Osb[:, 0:nq], Lsb[:, 0:nq], ALU.mult, [R(Osb), R(Lsb)], [R(t0b)])
        tt("pool", t1b[:, 0:nq], Osb[:, nq:n2], Lsb[:, nq:n2], ALU.mult, [R(Osb), R(Lsb)], [R(t1b)])
        stt("dve", ob[:, 0:nq], t1b[:, 0:nq], neglam[:, 0:1], t0b[:, 0:nq], ALU.mult, ALU.add,
            [R(t1b), R(t0b), R(neglam)], [R(ob)])
        tt("pool", osq[:, 0:nq], ob[:, 0:nq], ob[:, 0:nq], ALU.mult, [R(ob)], [R(osq)])
        mm(psb(EB)[:, 0:nq], ones[:, :], osq[:, 0:nq], True, True, [R(osq), R(ones)], EB)
        rsqrt_act(rsb, rsb[:, 0:nq], psb(EB)[:, 0:nq], [R(PS, EB)], lnb, lnb[:, 0:nq], 1.0 / 128)
        stt("dve", attnT[:, h, q0:q0 + nq], ob[:, 0:nq], gsub08[:, 0:1], rsb[:, 0:nq], ALU.mult, ALU.mult,
            [R(ob), R(gsub08), R(rsb)], [R(attnT, (h, q0))])

    load_head(0)
    for h in range(NH):
        if h + 1 < NH:
            load_head(h + 1)
        for m in range(ROWS):
            nkb = 8 * m + 8

            def mask_main(blk, m=m):
                if blk < 0 or blk < 8 * m:
                    return None
                i = blk - 8 * m
                return (mM[:, i, :], R(mM), (mM, i))
            attention_tile(h, m * BQ, BQ, nkb, 2, mask_main)

        def mask_halo(blk):
            if blk < 0:
                return (mH[0:16, 0, :], R(mH), None)
            return (mH[:, 1 + blk, :], R(mH), (mH, 1 + blk))
        attention_tile(h, NMAIN, NHQ, 64, 16, mask_halo)
        if dbg and h == 0:
            dma(dbgT["kT"], dbgT["kT"][:], KT[0], KT[0][:])
            dma(dbgT["v"], dbgT["v"][:], VV[0], VV[0][:].rearrange("p b e -> p (b e)"))
    if dbg:
        dma(dbgT["attnT"], dbgT["attnT"][:], attnT, attnT[:].rearrange("p h q -> p (h q)"))

    if stop == "C":
        return finish()
    P.barrier()
    TW = 256

    def wload(name, wb, lo, hi, kch):
        Wsb = A.alloc(name, [128, kch, hi - lo], BF16)
        dma(Wsb, Wsb[:], wb, w_view(wb, lo, hi), eng="pool")
        return Wsb

    A.off = CONST_END
    Wu = wload("Wu", w_in, 3072, 3584, 8)
    Wgrp = A.alloc("Wgrp", [128, 4, 128], BF16)
    dma(Wgrp, Wgrp[:], w_grp, w_grp[:].rearrange("g c d -> c g d"), eng="pool")
    Wpb = wload("Wpb", w_pool_br, 0, 1024, 4)
    uh = A.alloc("uh", [128, 4, NH17], F32)
    h1h = A.alloc("h1h", [128, 8, NHQ], F32)
    xbq = A.alloc("xbq", [128, 8, NHQ], BF16)
    xtq = A.alloc("xtq", [128, 8, NHQ], F32)
    r1q = A.alloc("r1q", [128, NHQ], F32)
    assert A.off <= QT_END
    A.off = ATT_END
    Wga = wload("Wga", w_in, 3584, 4608, 8)
    Wgp = wload("Wgp", w_in, 4608, 5632, 8)
    Wab = wload("Wab", w_attn_br, 0, 1024, 8)
    Wo = wload("Wo", w_out, 0, 1024, 8)
    xt1 = A.alloc("xt1", [128, 8, TW], F32)
    xb1 = A.alloc("xb1", [128, 8, TW], BF16)
    xq1 = A.alloc("xq1", [128, 8, TW], BF16)
    merged = xq1
    r1 = A.alloc("r1", [128, TW], F32)
    ln1 = A.alloc("ln1", [128, TW], F32)
    ub = A.alloc("ub", [128, 4, 271], F32)
    s1 = A.alloc("s1", [128, 4, 271], F32)
    s2 = A.alloc("s2", [128, 4, 271], F32)
    pooled = A.alloc("pooled", [128, 4, TW], BF16)
    mixed = A.alloc("mixed", [128, 4, TW], BF16)
    tg = [A.alloc("tg", [128, TW], F32) for _ in range(2)]
    m1 = [A.alloc("m1", [128, TW], F32) for _ in range(2)]
    m2 = [A.alloc("m2", [128, TW], F32) for _ in range(2)]
    gmix_bc = vec[:, G_MIX:G_MIX + 8].unsqueeze(2)

    def linear(Wsb, kch, o, rhs_of, n, reads):
        b = next_bank()
        for c in range(kch):
            mm(psb(b)[:, 0:n], Wsb[:, c, o * 128:(o + 1) * 128], rhs_of(c), c == 0, c == kch - 1, reads + [R(Wsb)], b)
        return b

    def norm_stats(xsq_b, xsq_ap_of, n, r_b, ln_b):
        b = next_bank()
        for c in range(8):
            mm(psb(b)[:, 0:n], ones[:, :], xsq_ap_of(c), c == 0, c == 7, [R(xsq_b), R(ones)], b)
        rsqrt_act(r_b, r_b[:, 0:n], psb(b)[:, 0:n], [R(PS, b)], ln_b, ln_b[:, 0:n], 1.0 / D)

    def d1_tile(kind, r):
        halo = kind == "halo"
        P.barrier()
        if halo:
            n_u, col0 = NH17, NMAIN
        else:
            n_u, col0 = TW, TW * r
        dma(xt1, xt1[:, :, 0:n_u], xT_own, xT_own[:].rearrange("(c p) t -> p c t", p=128)[:, :, col0:col0 + n_u])
        tt("dve", xb1[:, :, 0:n_u], xt1[:, :, 0:n_u], gmix_bc.to_broadcast([128, 8, n_u]), ALU.mult,
           [R(xt1), R(vec)], [R(xb1)])
        act(xq1[:, :, 0:n_u], xt1[:, :, 0:n_u], AF.Square, [R(xt1)], [R(xq1)])
        norm_stats(xq1, lambda c: xq1[:, c, 0:n_u], n_u, r1, ln1)
        if halo:
            for g in range(4):
                b = linear(Wu, 8, g, lambda c: xb1[:, c, 0:n_u], n_u, [R(xb1)])
                tt("dve", uh[:, g, :], psb(b)[:, 0:n_u], r1[:, 0:n_u], ALU.mult, [R(PS, b), R(r1)], [R(uh, g)])
            n = NHQ
            uv = uh[:].rearrange("p g (m k) -> p g m k", k=17)
            sa = s1[:].rearrange("p g k -> p (g k)")[:, 0:4 * NH17].rearrange("p (g m k) -> p g m k", g=4, k=17)
            sb2 = s2[:].rearrange("p g k -> p (g k)")[:, 0:4 * NH17].rearrange("p (g m k) -> p g m k", g=4, k=17)
            width, lo = 17, 15
            sl = lambda X, g0, a, b_: X[:, g0:4, :, a:b_]
            sg1 = lambda X, g, a, b_: X[:, g, :, a:b_]
            u_b = uh
        else:
            for g in range(4):
                b = linear(Wu, 8, g, lambda c: xb1[:, c, 0:TW], TW, [R(xb1)])
                tt("dve", ub[:, g, 15:271], psb(b)[:, 0:TW], r1[:, 0:TW], ALU.mult, [R(PS, b), R(r1)], [R(ub, g)])
            cp("pool", ub[:, :, 0:15], uh[:].rearrange("p g (m k) -> p g m k", k=17)[:, :, r, 2:17],
               [R(uh)], [R(ub, "h")])
            n = TW
            uv, sa, sb2 = ub[:], s1[:], s2[:]
            width, lo = 271, 15
            sl = lambda X, g0, a, b_: X[:, g0:4, a:b_]
            sg1 = lambda X, g, a, b_: X[:, g, a:b_]
            u_b = ub
        cur, cur_b = uv, u_b
        other = [(sa, s1), (sb2, s2)]
        oi = 0
        for lvl, step in enumerate((1, 2, 4, 8)):
            dst, dst_b = other[oi]
            oi ^= 1
            tt("dve", sl(dst, lvl, step, width), sl(cur, lvl, step, width), sl(cur, lvl, 0, width - step),
               ALU.add, [R(cur_b)], [R(dst_b)])
            wv = float(2 ** (lvl + 1))
            ssrc = sg1(dst, lvl, lo, width)
            usrc = sg1(uv, lvl, lo, width)
            if halo:
                pdst = pooled[:, lvl, 0:n].rearrange("p (m k) -> p m k", k=2)
                if lvl == 3:
                    tt("dve", ssrc, ssrc, c16[:].rearrange("p (m k) -> p m k", k=2), ALU.mult,
                       [R(dst_b), R(c16)], [R(dst_b)])
            else:
                pdst = pooled[:, lvl, :]
            stt("dve", pdst, usrc, -wv, ssrc, ALU.mult, ALU.add, [R(u_b), R(dst_b)], [R(pooled, lvl)])
            cur, cur_b = dst, dst_b
        if stop == "D1a" or (stop == "M0a" and not halo):
            raise _Stop()
        shp = lambda ap: ap
        if halo:
            v17 = lambda ap: ap.rearrange("p c (m k) -> p c m k", k=17)[:, :, :, 15:17]
            v2 = lambda ap: ap.rearrange("p c (m k) -> p c m k", k=2)
            cp("dve", v2(xbq[:]), v17(xb1[:, :, 0:NH17]), [R(xb1)], [R(xbq)])
            cp("dve", v2(xtq[:]), v17(xt1[:, :, 0:NH17]), [R(xt1)], [R(xtq)])
            cp("dve", r1q[:].rearrange("p (m k) -> p m k", k=2),
               r1[:, 0:NH17].rearrange("p (m k) -> p m k", k=17)[:, :, 15:17], [R(r1)], [R(r1q)])
            xcol = lambda c: xbq[:, c, :]
            xcol_b = xbq
            r1v, r1v_b = r1q[:, :], r1q
            xres, xres_b = (lambda o: xtq[:, o, :]), xtq
            qcols = slice(NMAIN, NMAIN + NHQ)
            h1o, h1o_b = (lambda o: h1h[:, o, :]), h1h
        else:
            xcol = lambda c: xb1[:, c, 0:TW]
            xcol_b = xb1
            r1v, r1v_b = r1[:, 0:TW], r1
            xres, xres_b = (lambda o: xt1[:, o, 0:TW]), xt1
            qcols = slice(TW * r, TW * r + TW)
            h1o, h1o_b = (lambda o: xt1[:, o, 0:TW]), xt1
        for g in range(4):
            b = next_bank()
            mm(psb(b)[:, 0:n], Wgrp[:, g, :], pooled[:, g, 0:n], True, True, [R(pooled, g), R(Wgrp)], b)
            act(mixed[:, g, 0:n], psb(b)[:, 0:n], AF.Copy, [R(PS, b), R(pscw)], [R(mixed, g)], scale=pscw[:, g:g + 1])
        if stop == "D1b" or (stop == "M0b" and not halo):
            raise _Stop()
        for o in range(8):
            q = o % 2
            b = linear(Wga, 8, o, xcol, n, [R(xcol_b)])
            tt("dve", shp(tg[q][:, 0:n]), shp(psb(b)[:, 0:n]), r1v, ALU.mult, [R(PS, b), R(r1v_b)], [R(tg[q])])
            act(tg[q][:, 0:n], tg[q][:, 0:n], AF.Sigmoid, [R(tg[q])], [R(tg[q])])
            b = linear(Wab, 8, o, lambda c: attnT[:, c, qcols], n, [R(attnT)])
            tt("dve", m1[q][:, 0:n], psb(b)[:, 0:n], tg[q][:, 0:n], ALU.mult, [R(PS, b), R(tg[q])], [R(m1[q])])
            b = linear(Wgp, 8, o, xcol, n, [R(xcol_b)])
            tt("dve", shp(tg[q][:, 0:n]), shp(psb(b)[:, 0:n]), r1v, ALU.mult, [R(PS, b), R(r1v_b)], [R(tg[q])])
            act(tg[q][:, 0:n], tg[q][:, 0:n], AF.Sigmoid, [R(tg[q])], [R(tg[q])])
            b = linear(Wpb, 4, o, lambda c: mixed[:, c, 0:n], n, [R(mixed)])
            tt("dve", m2[q][:, 0:n], psb(b)[:, 0:n], tg[q][:, 0:n], ALU.mult, [R(PS, b), R(tg[q])], [R(m2[q])])
            tt("pool", merged[:, o, 0:n], m1[q][:, 0:n], m2[q][:, 0:n], ALU.add, [R(m1[q]), R(m2[q])], [R(merged, o)])
        if stop == "D1c" or (stop == "M0c" and not halo):
            raise _Stop()
        for o in range(8):
            b = linear(Wo, 8, o, lambda c: merged[:, c, 0:n], n, [R(merged)])
            tt("dve", shp(h1o(o)), shp(psb(b)[:, 0:n]), xres(o), ALU.add, [R(PS, b), R(xres_b)], [R(h1o_b, ("o", o))])
        src = h1h[:] if halo else xt1[:, :, 0:TW]
        dma(h1_scr, h1_scr[:].rearrange("(c p) t -> p c t", p=128)[:, :, qcols], h1o_b, src, no_waw=True)

    try:
        d1_tile("halo", 0)
        if stop == "D1d":
            raise _Stop()
        for r in range(ROWS):
            d1_tile("main", r)
            if stop == "M0d":
                raise _Stop()
    except _Stop:
        return finish()

    if stop == "D1":
        return finish()
    P.barrier()
    A.off = CONST_END
    Wup = wload("Wup", w_up, 0, 5632, 8)
    Wdn = A.alloc("Wdn", [128, 22, 1024], BF16)
    dma(Wdn, Wdn[:], w_down, w_down[:].rearrange("(c p) n -> p c n", p=128), eng="pool")
    uph = A.alloc("uph", [128, NCH_FF, NHQ], F32)
    h1b = A.alloc("h1b", [128, 8, TW], F32)
    hb = A.alloc("hb", [128, 8, TW], BF16)
    r2 = A.alloc("r2", [128, TW], F32)
    ln2 = A.alloc("ln2", [128, TW], F32)
    upv = [A.alloc("upv", [128, 258], F32) for _ in range(2)]
    upg = [A.alloc("upg", [128, 258], F32) for _ in range(2)]
    yv = [A.alloc("yv", [128, TW], F32) for _ in range(2)]
    yg = [A.alloc("yg", [128, TW], F32) for _ in range(2)]
    actT = A.alloc("actT", [128, 22, TW], BF16)
    h2 = A.alloc("h2", [128, 8, TW], F32)

    def d2_tile(kind, r):
        halo = kind == "halo"
        P.barrier()
        if halo:
            n, qcols = NHQ, slice(NMAIN, NMAIN + NHQ)
        else:
            n, qcols = TW, slice(TW * r, TW * r + TW)
        dma(h1b, h1b[:, :, 0:n], h1_scr, h1_scr[:].rearrange("(c p) t -> p c t", p=128)[:, :, qcols])
        act(hb[:, :, 0:n], h1b[:, :, 0:n], AF.Square, [R(h1b)], [R(hb)])
        norm_stats(hb, lambda c: hb[:, c, 0:n], n, r2, ln2)
        tt("dve", h2[:, :, 0:n], h1b[:, :, 0:n], r2[:, 0:n].unsqueeze(1).to_broadcast([128, 8, n]), ALU.mult,
           [R(h1b), R(r2)], [R(h2)])
        tt("pool", hb[:, :, 0:n], h2[:, :, 0:n], vec[:, G_FFN:G_FFN + 8].unsqueeze(2).to_broadcast([128, 8, n]),
           ALU.mult, [R(h2), R(vec)], [R(hb)])
        if halo:
            for c in range(NCH_FF):
                b = linear(Wup, 8, c, lambda k: hb[:, k, 0:n], n, [R(hb)])
                act(uph[:, c, :], psb(b)[:, 0:n], AF.Copy, [R(PS, b)], [R(uph, c)])
            return
        for c in range(22):
            q = c % 2
            for (cc, ubuf, ybuf) in ((c, upv[q], yv[q]), (22 + c, upg[q], yg[q])):
                b = linear(Wup, 8, cc, lambda k: hb[:, k, 0:TW], TW, [R(hb)])
                w0 = vec[:, CW + 3 * cc + 0:CW + 3 * cc + 1]
                w1 = vec[:, CW + 3 * cc + 1:CW + 3 * cc + 2]
                w2 = vec[:, CW + 3 * cc + 2:CW + 3 * cc + 3]
                bb = vec[:, CB + cc:CB + cc + 1]
                act(ubuf[:, 2:258], psb(b)[:, 0:TW], AF.Copy, [R(PS, b)], [R(ubuf, "m")])
                act(ybuf[:], psb(b)[:, 0:TW], AF.Identity, [R(PS, b), R(vec)], [R(ybuf)], scale=w2, bias=bb)
                cp("pool", ubuf[:, 0:2], uph[:, cc, 2 * r:2 * r + 2], [R(uph)], [R(ubuf, "h")])
                ur = [R(ubuf, "m"), R(ubuf, "h"), R(vec)]
                stt("dve", ybuf[:], ubuf[:, 1:257], w1, ybuf[:], ALU.mult, ALU.add, ur + [R(ybuf)], [R(ybuf)])
                stt("dve", ybuf[:], ubuf[:, 0:256], w0, ybuf[:], ALU.mult, ALU.add, ur + [R(ybuf)], [R(ybuf)])
            act(yg[q][:], yg[q][:], AF.Silu, [R(yg[q])], [R(yg[q])])
            tt("dve", actT[:, c, :], yg[q][:], yv[q][:], ALU.mult, [R(yg[q]), R(yv[q])], [R(actT, c)])
        for o in range(8):
            b = linear(Wdn, 22, o, lambda k: actT[:, k, :], TW, [R(actT)])
            tt("dve", h2[:, o, :], psb(b)[:, 0:TW], h1b[:, o, 0:TW], ALU.add, [R(PS, b), R(h1b)], [R(h2, o)])
        act(hb[:], h2[:], AF.Square, [R(h2)], [R(hb)])
        norm_stats(hb, lambda c: hb[:, c, :], TW, r2, ln2)
        tt("dve", h2[:], h2[:], r2[:, 0:TW].unsqueeze(1).to_broadcast([128, 8, TW]), ALU.mult,
           [R(h2), R(r2)], [R(h2)])
        tt("pool", h2[:], h2[:], vec[:, G_FIN:G_FIN + 8].unsqueeze(2).to_broadcast([128, 8, TW]), ALU.mult,
           [R(h2), R(vec)], [R(h2)])
        dma(yT, yT[:].rearrange("(c p) t -> p c t", p=128)[:, :, qcols], h2, h2[:], no_waw=True)

    d2_tile("halo", 0)
    for r in range(ROWS):
        d2_tile("main", r)

    return finish()


_CACHE = {}


def _rope_tables(npos):
    pos = np.arange(npos, dtype=np.float32)
    inv = (1.0 / (np.float32(10000.0) ** (np.arange(0, 64, 2, dtype=np.float32) / np.float32(64)))).astype(np.float32)
    ang = (pos[:, None] * inv[None, :]).astype(np.float32)
    return np.cos(ang).astype(np.float32), np.sin(ang).astype(np.float32)


def _core_layout(j):
    main_p = np.zeros(NMAIN, np.int64)
    h17_p = np.zeros(NH17, np.int64)
    for m in range(ROWS):
        qb = 4 * m + j
        main_p[m * BQ:(m + 1) * BQ] = 16 + BQ * qb + np.arange(BQ)
        h17_p[m * 17:(m + 1) * 17] = 16 + BQ * qb - 17 + np.arange(17)
    hq_p = h17_p.reshape(ROWS, 17)[:, 15:17].reshape(-1)
    return main_p, h17_p, hq_p


def _prep_shared(inputs):
    f = lambda a: np.ascontiguousarray(np.asarray(a, dtype=np.float32))
    cosf, sinf = _rope_tables(NPOS)
    vecs = np.zeros((128, 205), np.float32)
    vecs[:, 0:8] = f(inputs["g_mix"])[0].reshape(8, 128).T
    vecs[:, 8:16] = f(inputs["g_ffn"])[0].reshape(8, 128).T
    vecs[:, 16:24] = f(inputs["g_final"]).reshape(8, 128).T
    vecs[:, 24:28] = f(inputs["pool_scale"])[0].reshape(4, 128).T
    vecs[:, 28] = f(inputs["g_subln"])[0]
    cw = f(inputs["conv_w"])[0]
    vecs[:, 29:29 + 132] = cw.reshape(3, NCH_FF, 128).transpose(2, 1, 0).reshape(128, 132)
    vecs[:, 161:205] = f(inputs["conv_b"])[0].reshape(NCH_FF, 128).T
    shared = {
        "cosK": cosf, "sinK": sinf,
        "w_in": f(inputs["w_in"])[0], "lam": f(inputs["lam"])[0].reshape(1, 256), "vecs": vecs,
        "w_grp": f(inputs["w_pool_grp"])[0], "w_attn_br": f(inputs["w_attn_br"])[0],
        "w_pool_br": f(inputs["w_pool_br"])[0], "w_out": f(inputs["w_out"])[0],
        "w_up": f(inputs["w_up"])[0], "w_down": f(inputs["w_down"])[0],
    }
    return shared, cosf, sinf


def _prep_core(c, x, metaT, xT_b, shared, cosf, sinf):
    j = c % 4
    main_p, h17_p, hq_p = _core_layout(j)
    xT_all = xT_b
    own_p = np.concatenate([main_p, h17_p, hq_p])
    valid = own_p >= 0
    xT_own = np.zeros((D, NOWN), np.float32)
    xT_own[:, valid] = xT_all[:, own_p[valid]]
    qpos = np.concatenate([main_p, hq_p])
    k = np.arange(128)[:, None, None]
    i = np.arange(8)[None, :, None]
    q = np.arange(BQ)[None, None, :]
    maskM = ((128 * i + k) <= (BQ * j + q)).astype(np.float32).reshape(128, 8 * BQ)
    keyp = np.full((65, 128), 1 << 30, np.int64)
    keyp[0, :16] = np.arange(16)
    keyp[1:, :] = 16 + 128 * np.arange(64)[:, None] + np.arange(128)[None, :]
    maskH = (keyp.T[:, :, None] <= hq_p[None, None, :]).astype(np.float32).reshape(128, 65 * NHQ)
    cnt16 = np.ones((128, NHQ), np.float32)
    cnt16[:, :] = (16.0 / np.minimum(16, hq_p + 1).astype(np.float64)).astype(np.float32)[None, :]
    m = dict(shared)
    m.update({
        "xT_all": xT_all, "xT_own": xT_own,
        "cosQ": np.ascontiguousarray(cosf[qpos]), "sinQ": np.ascontiguousarray(sinf[qpos]),
        "maskM": maskM, "maskH": maskH, "cnt16": cnt16,
    })
    return m


def kernel(**inputs):
    dbg = bool(inputs.pop("_dbg", False))
    stop = inputs.pop("_stop", None)
    cores = inputs.pop("_cores", list(range(8)))
    x = np.asarray(inputs["x"], dtype=np.float32)
    meta = np.asarray(inputs["meta_tokens"], dtype=np.float32)
    key = ("prog", dbg, stop)
    if key not in _CACHE:
        _CACHE[key] = build_program(dbg, stop)[0]
    nc = _CACHE[key]
    shared, cosf, sinf = _prep_shared(inputs)
    xT = [np.ascontiguousarray(np.concatenate([meta, x[b]], axis=0).T) for b in range(2)]
    in_maps = [_prep_core(c, x, None, xT[c // 4], shared, cosf, sinf) for c in cores]
    res = run_bass_kernel_spmd(nc, in_maps, core_ids=list(range(len(cores))))
    out = np.zeros((2, SEQ, D), np.float32)
    for ci, c in enumerate(cores):
        b, j = c // 4, c % 4
        yT = np.asarray(res.results[ci]["yT"])
        for m in range(ROWS):
            qb = 4 * m + j
            out[b, BQ * qb:BQ * (qb + 1), :] = yT[:, m * BQ:(m + 1) * BQ].T
    if dbg:
        return out, res
    return out
```
